# Optimizing a Trainium2 kernel written in Bass

```python
import math
import jax
import jax.numpy as jnp
from jax import lax
import numpy as np

D_MODEL = 1024
BATCH = 8
SEQ = 2048
DEPTH = 4
DEC_BATCH = 32
DEC_SEQ = 1
PAST_LEN = 8192
PAGE_SIZE = 128

HEAD_DIM = 64
HEADS_PER_GROUP = 4
ATTN_GROUPS = ((128, 1), (512, 4), (2048, 16))
N_ATTN_HEADS = HEADS_PER_GROUP * len(ATTN_GROUPS)
ATTN_QKV_WIDTH = N_ATTN_HEADS * HEAD_DIM
ATTN_OUT_WIDTH = HEADS_PER_GROUP * HEAD_DIM
ROT_DIM = HEAD_DIM // 4
ROPE_THETA = 500000.0
BAND_BLOCK = 128
CHUNK = 128
GMLP_GROUPS = 4
GMLP_GROUP_DIM = 192
GMLP_WIDTH = GMLP_GROUPS * GMLP_GROUP_DIM
EPS = 1e-6
IN_WIDTHS = (ATTN_QKV_WIDTH, ATTN_QKV_WIDTH, ATTN_QKV_WIDTH, ATTN_OUT_WIDTH, GMLP_WIDTH, GMLP_WIDTH, GMLP_WIDTH, 2 * D_MODEL)
N_IN = sum(IN_WIDTHS)
SPLITS = tuple(sum(IN_WIDTHS[:i + 1]) for i in range(len(IN_WIDTHS) - 1))

kernel_name = 'hybrid_dilated_attn_gmlp_decode_step'


def rms_norm(x, g):
    xf = x.astype(jnp.float32)
    y = xf * lax.rsqrt(jnp.mean(xf * xf, axis=-1, keepdims=True) + EPS)
    return (y * g.astype(jnp.float32)).astype(x.dtype)


def layer_norm(x, g, b):
    xf = x.astype(jnp.float32)
    xc = xf - jnp.mean(xf, axis=-1, keepdims=True)
    y = xc * lax.rsqrt(jnp.mean(xc * xc, axis=-1, keepdims=True) + EPS)
    return (y * g.astype(jnp.float32) + b.astype(jnp.float32)).astype(x.dtype)


def rope_partial(x, pos):
    half = ROT_DIM // 2
    inv_freq = jnp.power(ROPE_THETA, -jnp.arange(half, dtype=jnp.float32) / half)
    ang = pos.astype(jnp.float32)[:, None] * inv_freq[None, :]
    cos = jnp.cos(ang)[:, None, :]
    sin = jnp.sin(ang)[:, None, :]
    xr = x[..., :ROT_DIM].astype(jnp.float32)
    x1, x2 = xr[..., :half], xr[..., half:]
    rot = jnp.concatenate([x1 * cos - x2 * sin, x2 * cos + x1 * sin], axis=-1)
    return jnp.concatenate([rot.astype(x.dtype), x[..., ROT_DIM:]], axis=-1)


def dilated_band_prompt(q, k, v, window, dilation):
    B, S, H, hd = q.shape
    n = S // dilation
    span = window // dilation
    qlen = math.gcd(n, BAND_BLOCK)
    nblk = n // qlen

    def to_sub(t):
        return t.reshape(B, n, dilation, H, hd).transpose(0, 2, 1, 3, 4)

    qs = to_sub(q).reshape(B, dilation, nblk, qlen, H, hd)
    pad = ((0, 0), (0, 0), (span, 0), (0, 0), (0, 0))
    kp = jnp.pad(to_sub(k), pad)
    vp = jnp.pad(to_sub(v), pad)
    kidx = jnp.arange(nblk)[:, None] * qlen + jnp.arange(qlen + span)[None, :]
    kb = kp[:, :, kidx]
    vb = vp[:, :, kidx]
    blk = jnp.arange(nblk)[:, None, None]
    qi = jnp.arange(qlen)[None, :, None]
    kj = jnp.arange(qlen + span)[None, None, :]
    dist = qi - kj + span
    valid = (dist >= 0) & (dist <= span) & (blk * qlen + kj >= span)
    scores = jnp.einsum('brnqhd,brnkhd->brnhqk', qs, kb, preferred_element_type=jnp.float32) * (hd ** -0.5)
    scores = jnp.where(valid[:, None], scores, -jnp.inf)
    lse = jax.nn.logsumexp(scores, axis=-1)
    p = jnp.exp(scores - lse[..., None]).astype(v.dtype)
    o = jnp.einsum('brnhqk,brnkhd->brnqhd', p, vb)
    o = o.reshape(B, dilation, n, H, hd).transpose(0, 2, 1, 3, 4).reshape(B, S, H, hd)
    lse = jnp.swapaxes(lse, -1, -2).reshape(B, dilation, n, H).transpose(0, 2, 1, 3).reshape(B, S, H)
    return o, lse


def dilated_band_sample(q, k, v, kv_cache, window, dilation):
    L = kv_cache.shape[1]
    T = q.shape[1]
    hd = q.shape[-1]
    span = window // dilation
    kc = jnp.concatenate([kv_cache[:, :, 0], k], axis=1)
    vc = jnp.concatenate([kv_cache[:, :, 1], v], axis=1)
    rows = L + jnp.arange(T)[:, None] - dilation * jnp.arange(span + 1)[None, :]
    valid = rows >= 0
    rows = jnp.maximum(rows, 0)
    kg = kc[:, rows]
    vg = vc[:, rows]
    scores = jnp.einsum('bthd,btjhd->bhtj', q, kg, preferred_element_type=jnp.float32) * (hd ** -0.5)
    scores = jnp.where(valid[None, None], scores, -jnp.inf)
    lse = jax.nn.logsumexp(scores, axis=-1)
    p = jnp.exp(scores - lse[..., None]).astype(v.dtype)
    o = jnp.einsum('bhtj,btjhd->bthd', p, vg)
    return o, jnp.swapaxes(lse, 1, 2)


def layer_step(x, pos, kv_caches, g_pre, w_in, b_merge, v_norm_g, v_norm_b, w_s, b_s, w_pa, w_pb, w_out, g_post):
    B, T, _ = x.shape
    z = rms_norm(x, g_pre) @ w_in
    q, k, v, gate_a, u, vb, gate_b, merge_logits = jnp.split(z, SPLITS, axis=-1)
    q = rope_partial(q.reshape(B, T, N_ATTN_HEADS, HEAD_DIM), pos)
    k = rope_partial(k.reshape(B, T, N_ATTN_HEADS, HEAD_DIM), pos)
    v = v.reshape(B, T, N_ATTN_HEADS, HEAD_DIM)
    outs, lses, kv_rows = [], [], []
    for g, (window, dilation) in enumerate(ATTN_GROUPS):
        hs = slice(g * HEADS_PER_GROUP, (g + 1) * HEADS_PER_GROUP)
        qg, kg, vg = q[:, :, hs], k[:, :, hs], v[:, :, hs]
        if kv_caches is None:
            o, lse = dilated_band_prompt(qg, kg, vg, window, dilation)
            keep = min(window, T)
            kv_rows.append(jnp.stack([kg[:, T - keep:], vg[:, T - keep:]], axis=2))
        else:
            o, lse = dilated_band_sample(qg, kg, vg, kv_caches[g], window, dilation)
            kv_rows.append(jnp.stack([kg, vg], axis=2))
        outs.append(o)
        lses.append(lse)
    alpha = jax.nn.softmax(jnp.stack(lses, axis=0), axis=0)[..., None]
    attn = jnp.sum(alpha * jnp.stack(outs, axis=0).astype(jnp.float32), axis=0)
    attn = attn.reshape(B, T, ATTN_OUT_WIDTH).astype(x.dtype)
    u = jax.nn.gelu(u)
    vn = layer_norm(jax.nn.gelu(vb), v_norm_g, v_norm_b)
    cl = CHUNK if kv_caches is None else T
    ws = jnp.where(jnp.tril(jnp.ones((CHUNK, CHUNK), dtype=bool)), w_s, 0)[:, :cl, :cl]
    vc = vn.reshape(B, T // cl, cl, GMLP_GROUPS, GMLP_GROUP_DIM)
    mix = jnp.einsum('gij,bnjgc->bnigc', ws, vc) + jnp.swapaxes(b_s[:, :cl], 0, 1)[:, :, None]
    sgu = u * mix.reshape(B, T, GMLP_WIDTH)
    branch_a = (attn * jax.nn.silu(gate_a)) @ w_pa
    branch_b = (sgu * jax.nn.silu(gate_b)) @ w_pb
    gates = jax.nn.sigmoid(merge_logits + b_merge)
    merged = gates[..., :D_MODEL] * branch_a + gates[..., D_MODEL:] * branch_b
    x = x + rms_norm(merged @ w_out, g_post)
    return x, kv_rows, vn


def setup_inputs(seed: int = 0) -> dict:
    key = jax.random.key(seed)
    ks = jax.random.split(key, 16)

    def nrm(k, shape, scale=1.0):
        return scale * jax.random.normal(k, shape, jnp.float32)

    def cache_shape(window):
        return (DEPTH, DEC_BATCH, min(window, PAST_LEN), 2, HEADS_PER_GROUP, HEAD_DIM)

    return {
        'x_prompt': nrm(ks[0], (BATCH, SEQ, D_MODEL)),
        'x_sample': nrm(ks[1], (DEC_BATCH, DEC_SEQ, D_MODEL)),
        'cache_kv_w128': nrm(ks[2], cache_shape(ATTN_GROUPS[0][0])),
        'cache_kv_w512': nrm(ks[3], cache_shape(ATTN_GROUPS[1][0])),
        'cache_kv_w2048': nrm(ks[4], cache_shape(ATTN_GROUPS[2][0])),
        'norm_pre': 1.0 + nrm(ks[5], (DEPTH, D_MODEL), 0.02),
        'w_in': nrm(ks[6], (DEPTH, D_MODEL, N_IN), D_MODEL ** -0.5),
        'b_merge': nrm(ks[7], (DEPTH, 2 * D_MODEL), 0.02),
        'v_norm_g': 1.0 + nrm(ks[8], (DEPTH, GMLP_WIDTH), 0.02),
        'v_norm_b': nrm(ks[9], (DEPTH, GMLP_WIDTH), 0.02),
        'w_spatial': nrm(ks[10], (DEPTH, GMLP_GROUPS, CHUNK, CHUNK), CHUNK ** -0.5),
        'b_spatial': 1.0 + nrm(ks[11], (DEPTH, GMLP_GROUPS, CHUNK), 0.02),
        'w_proj_a': nrm(ks[12], (DEPTH, ATTN_OUT_WIDTH, D_MODEL), ATTN_OUT_WIDTH ** -0.5),
        'w_proj_b': nrm(ks[13], (DEPTH, GMLP_WIDTH, D_MODEL), GMLP_WIDTH ** -0.5),
        'w_out': nrm(ks[14], (DEPTH, D_MODEL, D_MODEL), D_MODEL ** -0.5),
        'norm_post': 1.0 + nrm(ks[15], (DEPTH, D_MODEL), 0.02),
    }


def reference(x_prompt, x_sample, cache_kv_w128, cache_kv_w512, cache_kv_w2048, norm_pre, w_in, b_merge, v_norm_g, v_norm_b, w_spatial, b_spatial, w_proj_a, w_proj_b, w_out, norm_post):
    pos_prompt = jnp.arange(x_prompt.shape[1], dtype=jnp.int32)
    pos_sample = PAST_LEN + jnp.arange(x_sample.shape[1], dtype=jnp.int32)
    caches = (cache_kv_w128, cache_kv_w512, cache_kv_w2048)
    xp, xs = x_prompt, x_sample
    kv_p = [[] for _ in ATTN_GROUPS]
    kv_s = [[] for _ in ATTN_GROUPS]
    v_s = []
    for l in range(DEPTH):
        w = (norm_pre[l], w_in[l], b_merge[l], v_norm_g[l], v_norm_b[l], w_spatial[l], b_spatial[l], w_proj_a[l], w_proj_b[l], w_out[l], norm_post[l])
        xp, rows_p, _ = layer_step(xp, pos_prompt, None, *w)
        xs, rows_s, vn_s = layer_step(xs, pos_sample, [c[l] for c in caches], *w)
        for g in range(len(ATTN_GROUPS)):
            kv_p[g].append(rows_p[g])
            kv_s[g].append(rows_s[g])
        v_s.append(vn_s)
    new_kv_w128_prompt = jnp.stack(kv_p[0], axis=0)
    new_kv_w512_prompt = jnp.stack(kv_p[1], axis=0)
    new_kv_w2048_prompt = jnp.stack(kv_p[2], axis=0)
    new_kv_w128_sample = jnp.stack(kv_s[0], axis=0)
    new_kv_w512_sample = jnp.stack(kv_s[1], axis=0)
    new_kv_w2048_sample = jnp.stack(kv_s[2], axis=0)
    new_gmlp_v_sample = jnp.stack(v_s, axis=0)
    return (xp, xs, new_kv_w128_prompt, new_kv_w512_prompt, new_kv_w2048_prompt, new_kv_w128_sample, new_kv_w512_sample, new_kv_w2048_sample, new_gmlp_v_sample)
```

```python
import contextlib
import math
import numpy as np
import concourse.bass as bass
import concourse.mybir as mybir
from concourse.bass_utils import run_bass_kernel_spmd

F32 = mybir.dt.float32
BF16 = mybir.dt.bfloat16
AF = mybir.ActivationFunctionType
ALU = mybir.AluOpType
AX = mybir.AxisListType

DEPTH = 4
S = 2048
D = 1024
NB = 4
N_IN = 6912
EPS = 1e-6
DIL = (1, 4, 16)
SEM_LIMIT = 16000

OFF_A = 0
OFF_GA = 2304
OFF_VB = 2560
OFF_U = 3328
OFF_GB = 4096
OFF_E = 4864


class _Eng:
    def __init__(self, fw, name, h):
        self.fw, self.name, self.h = fw, name, h
        self.sem = None
        self.val = 0
        self.nsem = 0
        self.waited = {}

    def bump(self):
        if self.sem is None or self.val + 1 > SEM_LIMIT:
            self.sem = self.fw.new_sem(f"e_{self.name}_{self.nsem}")
            self.nsem += 1
            self.val = 0
        self.val += 1
        return (self.sem, self.val, self.name)


class _DmaSem:
    def __init__(self, name):
        self.name = name
        self.sem = None
        self.val = 0
        self.nsem = 0


class _Res:
    __slots__ = ("w", "r")

    def __init__(self):
        self.w = None
        self.r = {}


class FW:
    def __init__(self, nc):
        self.nc = nc
        self.stack = contextlib.ExitStack()
        self.res = {}
        self.dsem = {}
        self.nsems = 0
        self.out_events = {}
        self.pe = _Eng(self, "pe", nc.tensor)
        self.act = _Eng(self, "act", nc.scalar)
        self.dve = _Eng(self, "dve", nc.vector)
        self.pool = _Eng(self, "pool", nc.gpsimd)
        self.sp = _Eng(self, "sp", nc.sync)
        self.ninstr = {"pe": 0, "act": 0, "dve": 0, "pool": 0, "sp": 0, "wait": 0}

    def new_sem(self, name):
        self.nsems += 1
        return self.stack.enter_context(self.nc.semaphore(name))

    def sbuf(self, name, shape, dt):
        return self.stack.enter_context(self.nc.sbuf_tensor("sb_" + name, shape, dt))

    def psum(self, name, shape, dt):
        return self.stack.enter_context(self.nc.psum_tensor("pp_" + name, shape, dt))

    def _r(self, k):
        r = self.res.get(k)
        if r is None:
            r = self.res[k] = _Res()
        return r

    def _need(self, eng, ev, kind):
        sem, val, src = ev
        if src == eng.name and (kind != "RAW" or src == "pe"):
            return
        key = id(sem)
        if eng.waited.get(key, 0) >= val:
            return
        eng.h.wait_ge(sem, val)
        self.ninstr["wait"] += 1
        eng.waited[key] = val

    def _pre(self, eng, reads, writes):
        for k in reads:
            r = self._r(k)
            if r.w is not None:
                self._need(eng, r.w, "RAW")
        for k in writes:
            r = self._r(k)
            if r.w is not None:
                self._need(eng, r.w, "WAW")
            for e in r.r.values():
                self._need(eng, e, "WAR")

    def _post(self, ev, reads, writes):
        for k in reads:
            self._r(k).r[id(ev[0])] = ev
        for k in writes:
            r = self._r(k)
            r.w = ev
            r.r = {}

    def sync_on(self, eng, keys):
        for k in keys:
            r = self.res.get(k)
            if r is None:
                continue
            if r.w is not None:
                self._need(eng, r.w, "RAW")
            for e in r.r.values():
                self._need(eng, e, "RAW")

    def op(self, eng, reads, writes, fn):
        self._pre(eng, reads, writes)
        ins = fn()
        ev = eng.bump()
        ins.then_inc(ev[0], 1)
        self.ninstr[eng.name] += 1
        self._post(ev, reads, writes)
        return ev

    def dma(self, q, out, in_, reads, writes, semkey=None, is_output=False):
        if semkey is None:
            semkey = writes[0] if writes else reads[0]
        ds = self.dsem.get(semkey)
        if ds is None:
            ds = self.dsem[semkey] = _DmaSem("d%d" % len(self.dsem))
        self._pre(q, reads, writes)
        if ds.sem is None or ds.val + 16 > SEM_LIMIT:
            if ds.sem is not None:
                self._need(q, (ds.sem, ds.val, "dma"), "RAW")
            ds.sem = self.new_sem(f"{ds.name}_{ds.nsem}")
            ds.nsem += 1
            ds.val = 0
        ins = q.h.dma_start(out=out, in_=in_)
        ds.val += 16
        ev = (ds.sem, ds.val, "dma")
        ins.then_inc(ds.sem, 16)
        self.ninstr[q.name] += 1
        self._post(ev, reads, writes)
        if is_output:
            self.out_events[id(ds.sem)] = ev
        return ev

    def finish(self):
        for ev in list(self.out_events.values()):
            self._need(self.sp, ev, "RAW")
        for e in (self.pe, self.act, self.dve, self.pool):
            if e.sem is not None:
                self._need(self.sp, (e.sem, e.val, e.name), "RAW")

    def close(self):
        self.stack.close()


class Rot:
    def __init__(self, n):
        self.n, self.i = n, -1

    def next(self):
        self.i = (self.i + 1) % self.n
        return self.i


def build_program(nlayers=DEPTH):
    nc = bass.Bass("TRN2", target_bir_lowering=False)

    def din(name, shape):
        return nc.dram_tensor(name, list(shape), F32, kind="ExternalInput").ap()

    def dout(name, shape):
        return nc.dram_tensor(name, list(shape), F32, kind="ExternalOutput").ap()

    xp = din("xp", [S, D])
    xs_in = din("xs", [NB, D])
    caches = [din("c128", [DEPTH, NB, 128, 512]), din("c512", [DEPTH, NB, 512, 512]),
              din("c2048", [DEPTH, NB, 2048, 512])]
    w_in = din("w_in", [DEPTH, 128, 8, N_IN])
    wpa_d = din("wpa", [DEPTH, 128, 2, D])
    wpb_d = din("wpb", [DEPTH, 128, 6, D])
    wout_d = din("wout", [DEPTH, 128, 8, D])
    gpreT_d = din("gpreT", [128, DEPTH, 8])
    gpost_d = din("gpost", [DEPTH, D])
    bm_d = din("bm", [128, DEPTH, 16])
    gam_d = din("gam", [DEPTH, 768])
    beta_d = din("beta", [DEPTH, 768])
    wsT_d = din("wsT", [DEPTH, 128, 4, 128])
    bsT_d = din("bsT", [DEPTH, 128, 6, 128])
    ws00_d = din("ws00", [DEPTH * 4])
    bs0_d = din("bs0", [DEPTH * 4])
    ident_d = din("ident", [128, 128])
    mask_d = din("mask", [128, 256])
    maskbias_d = din("maskbias", [128, 256])
    cs_d = din("cs", [3, 128, 16, 32])
    css_d = din("css", [NB, 32])
    sel_d = din("sel", [NB, NB, 128])
    oh_d = din("oh", [128, NB, NB])

    y_p = dout("y_p", [S, D])
    y_s = dout("y_s", [NB, D])
    kvp = [dout("kv128p", [DEPTH, 128, 512]), dout("kv512p", [DEPTH, 512, 512]),
           dout("kv2048p", [DEPTH, 2048, 512])]
    kvs = [dout("kv128s", [DEPTH, NB, 512]), dout("kv512s", [DEPTH, NB, 512]),
           dout("kv2048s", [DEPTH, NB, 512])]
    gv_s = dout("gv_s", [DEPTH, NB, 768])
    xscr = nc.dram_tensor("xscr", [S, D], F32).ap()
    qscr = nc.dram_tensor("qscr", [DEPTH, NB, 768], F32).ap()

    f = FW(nc)
    PE, ACT, DVE, POOL, SP = f.pe, f.act, f.dve, f.pool, f.sp

    hT = f.sbuf("hT", [128, 8, S], BF16)
    hTs = f.sbuf("hTs", [128, 8, NB], BF16)
    Wc = [f.sbuf(f"Wc{i}", [128, 8, 768], BF16) for i in range(3)]
    wpa = f.sbuf("wpa", [128, 2, D], BF16)
    PB = f.sbuf("PB", [128, 32768], BF16)
    stg = [f.sbuf(f"stg{i}", [128, 1024], F32) for i in range(5)]
    hb = [f.sbuf(f"hb{i}", [128, 1024], BF16) for i in range(2)]
    junk = f.sbuf("junk", [128, 1024], BF16)
    gpost = f.sbuf("gpost", [128, D], F32)
    gam = f.sbuf("gamt", [128, 768], F32)
    beta = f.sbuf("betat", [128, 768], F32)
    bsT = f.sbuf("bsTt", [128, 6, 128], F32)
    wsT32 = f.sbuf("wsT32", [128, 4, 128], F32)
    wsT = f.sbuf("wsTb", [128, 4, 128], BF16)
    maskb = f.sbuf("maskb", [128, 256], BF16)
    mask32 = f.sbuf("mask32", [128, 128], F32)
    identb = f.sbuf("identb", [128, 128], BF16)
    ones64 = f.sbuf("ones64", [128, 64], BF16)
    css = f.sbuf("csst", [NB, 32], F32)
    gpreT = f.sbuf("gpreTt", [128, DEPTH, 8], F32)
    bm = f.sbuf("bmt", [128, DEPTH, 16], F32)
    ws00 = f.sbuf("ws00t", [NB, DEPTH * 4], F32)
    bs0 = f.sbuf("bs0t", [NB, DEPTH * 4], F32)
    sel = f.sbuf("selt", [NB, NB, 128], F32)
    oh = f.sbuf("oht", [128, NB, NB], F32)
    epsT = f.sbuf("epsT", [128, 1], F32)
    NACC = 320
    NSTAT = NACC + 128 + 96
    stat = f.sbuf("stat", [128, NSTAT], F32)
    xs_t = f.sbuf("xs_t", [NB, D], F32)
    zs = f.sbuf("zs", [NB, 2304], F32)
    nd = f.sbuf("nd", [NB, 800], F32)
    sga = f.sbuf("sga", [NB, 256], F32)
    sm = f.sbuf("sm", [NB, 512], F32)
    hsb = f.sbuf("hsb", [NB, D], BF16)
    gaTs = f.sbuf("gaTs", [128, 2, NB], BF16)
    ugTs = f.sbuf("ugTs", [128, 6, NB], BF16)
    mTs = f.sbuf("mTs", [128, 8, NB], BF16)

    gaT = PB[:, 0:4096].rearrange("p (a t) -> p a t", a=2)
    qkT = PB[:, 4096:16384].rearrange("p (g c t) -> p g c t", g=3, c=2)
    Vt = PB[:, 16384:22528].rearrange("p (g t e) -> p g t e", g=3, t=16)
    accO = PB[:, 22528:26624].bitcast(F32)
    rec = PB[:, 26624:28672].bitcast(F32).rearrange("p (a t) -> p a t", a=2)
    Eb = PB[:, 28672:29696].rearrange("p (a t) -> p a t", a=4)
    qkb = PB[:, 29696:30720].rearrange("p (a t) -> p a t", a=4)
    cs2 = [PB[:, 30720 + i * 1024:30720 + (i + 1) * 1024].bitcast(F32).rearrange("p (t e) -> p t e", t=16) for i in range(2)]
    hbF = [PB[:, i * 1024:(i + 1) * 1024] for i in range(4)]
    ugT = PB[:, 4096:16384].rearrange("p (a t) -> p a t", a=6)
    vn = PB[:, 16384:28672].rearrange("p (t e) -> p t e", t=16)
    mT = PB[:, 16384:32768].rearrange("p (a t) -> p a t", a=8)
    tmpb = PB[:, 30720:31744].rearrange("p (a t) -> p a t", a=2)

    psA = [f.psum(f"psA{i}", [128, 1024], F32) for i in range(3)]
    psT = [f.psum(f"psT{i}", [128, 1024], BF16) for i in range(2)]

    def bank(i):
        return psA[i // 2][:, (i % 2) * 512:(i % 2) * 512 + 512]

    def bkeys(i):
        return [("ph", i, 0), ("ph", i, 1)]

    rot_b = Rot(6)
    rot_d = Rot(3)
    rot_t = Rot(2)
    rot_stg = Rot(5)
    rot_hb = Rot(2)
    rot_wc = Rot(3)
    stat_i = [0]
    rot_s1 = Rot(128)
    rot_s12 = Rot(8)

    def acc_stat(n=1):
        i = stat_i[0]
        stat_i[0] += n
        assert stat_i[0] <= NACC
        return i

    def new_stat(n=1):
        if n == 1:
            return NACC + rot_s1.next()
        assert n <= 12
        return NACC + 128 + 12 * rot_s12.next()

    f.dma(SP, gpreT[:], gpreT_d, [], ["gpreT"])
    f.dma(SP, bm[:], bm_d, [], ["bm"])
    f.dma(SP, mask32[:], mask_d[:, 128:256], [], ["mask32"])
    f.dma(SP, css[:], css_d, [], ["css"])
    f.dma(SP, sel[:], sel_d, [], ["sel"])
    f.dma(SP, oh[:], oh_d, [], ["oh"])
    f.dma(SP, ws00[:], ws00_d.partition_broadcast(NB), [], ["ws00"])
    f.dma(SP, bs0[:], bs0_d.partition_broadcast(NB), [], ["bs0"])
    f.dma(SP, xs_t[:], xs_in, [], ["xs_t"])
    f.dma(POOL, identb[:], ident_d, [], ["identb"])
    f.dma(POOL, maskb[:], mask_d, [], ["maskb"])
    f.op(DVE, [], ["ones64"], lambda: nc.vector.memset(ones64[:], 1.0))
    f.op(DVE, [], ["epsT"], lambda: nc.vector.memset(epsT[:], EPS))
    f.op(DVE, [], ["stat"], lambda: nc.vector.memset(stat[:], 0.0))

    preloaded = {}

    def _issue_chunk(l, off, ncols, avoid=()):
        i = rot_wc.next()
        while i in avoid:
            i = rot_wc.next()
        f.dma(POOL, Wc[i][:, :, 0:ncols], w_in[l, :, :, off:off + ncols], [], [("Wc", i)])
        return i

    def load_chunk(l, off, ncols, avoid=()):
        if (l, off) in preloaded:
            return preloaded.pop((l, off))
        return _issue_chunk(l, off, ncols, avoid)

    def preload_chunk(l, off, ncols, avoid=()):
        if l < nlayers:
            preloaded[(l, off)] = _issue_chunk(l, off, ncols, avoid)

    def rstd_from_ss(ss_slot, scale, P, n=1):
        sd = new_stat(n)
        rs = new_stat(n)
        f.op(ACT, [("st", ss_slot), "epsT"], [("st", sd)],
             lambda: nc.scalar.activation(stat[0:P, sd:sd + n], stat[0:P, ss_slot:ss_slot + n], AF.Sqrt,
                                          bias=epsT[0:P, 0:1], scale=scale))
        f.op(DVE, [("st", sd)], [("st", rs)],
             lambda: nc.vector.reciprocal(stat[0:P, rs:rs + n], stat[0:P, sd:sd + n]))
        return rs

    rot_hbF = Rot(4)

    def prenorm_a(x_ap, xkey, P):
        ss = acc_stat()
        f.op(ACT, [xkey, "stat"], [("st", ss), "junk"],
             lambda: nc.scalar.activation(junk[0:P, :], x_ap, AF.Square, accum_out=stat[0:P, ss:ss + 1]))
        rs = rstd_from_ss(ss, 1.0 / D, P)
        if P == 128:
            hi = rot_hbF.next()
            hap, hkey = hbF[hi], ("hbF", hi)
        else:
            hap, hkey = hsb[:, :], "hsb"
        f.op(ACT, [xkey, ("st", rs)], [hkey],
             lambda: nc.scalar.activation(hap[0:P, :], x_ap, AF.Copy, scale=stat[0:P, rs:rs + 1]))
        return hap, hkey

    def prenorm_b(l, hh_, P, hT_dst, hT_key):
        hap, hkey = hh_
        ti = rot_t.next()

        def tr():
            for kt in range(8):
                ins = nc.tensor.transpose(psT[ti][:, kt * P:(kt + 1) * P], hap[0:P, kt * 128:(kt + 1) * 128],
                                          identb[0:P, 0:P])
            return ins
        f.op(PE, [hkey, "identb"], [("pt", ti)], tr)
        f.op(DVE, [("pt", ti), "gpreT"], [hT_key],
             lambda: nc.vector.tensor_tensor(hT_dst, psT[ti][:, 0:8 * P].rearrange("p (k t) -> p k t", k=8),
                                             gpreT[:, l, :].unsqueeze(2).to_broadcast([128, 8, P]), ALU.mult))

    def prenorm(l, x_ap, xkey, P, hT_dst, hT_key):
        hi = prenorm_a(x_ap, xkey, P)
        prenorm_b(l, hi, P, hT_dst, hT_key)

    def load_layer_small(l):
        f.dma(SP, gam[:], gam_d[l].partition_broadcast(128), [], ["gam"])
        f.dma(SP, beta[:], beta_d[l].partition_broadcast(128), [], ["beta"])
        f.dma(SP, bsT[:], bsT_d[l], [], ["bsT"])
        f.dma(SP, wsT32[:], wsT_d[l], [], ["wsT32"])

    hT_keys = [("hT", t) for t in range(16)]
    ZK = [("zs", c) for c in range(6)]
    MT_KEYS = [("mT", a) for a in range(8)]
    UG_KEYS = [("ugT", a, n) for a in range(6) for n in range(4)]

    for tt in range(16):
        si = rot_stg.next()
        f.dma(SP, stg[si][:], xp[tt * 128:(tt + 1) * 128, :], [], [("stg", si)])
        prenorm(0, stg[si][:], ("stg", si), 128, hT[:, :, tt * 128:(tt + 1) * 128], ("hT", tt))
    prenorm(0, xs_t[:], "xs_t", NB, hTs[:], "hTs")

    for l in range(nlayers):
        last = l == nlayers - 1
        load_layer_small(l)
        HBF_KEYS = [("hbF", i) for i in range(4)]
        f.sync_on(ACT, MT_KEYS + UG_KEYS + HBF_KEYS)
        f.sync_on(DVE, MT_KEYS + UG_KEYS + HBF_KEYS)
        f.sync_on(SP, MT_KEYS)

        for sp in range(2):
            for g in range(3):
                dl = DIL[g]
                nblk = (S // dl) // 128
                ci = load_chunk(l, OFF_A + (sp * 3 + g) * 384, 384)
                csi = (sp * 3 + g) % 2
                cs = cs2[csi]
                cskey = ("cs", csi)
                f.dma(SP, cs, cs_d[g], [], [cskey])

                st1 = {}

                def stage1(pt):
                    r, mb = divmod(pt, nblk)
                    start = r + dl * mb * 128
                    stop = start + dl * 127 + 1
                    bi = rot_b.next()
                    rkeys = list(hT_keys) if dl > 1 else [("hT", pt)]

                    def mmA():
                        for kt in range(8):
                            ins = nc.tensor.matmul(bank(bi)[:, 0:384], hT[:, kt, start:stop:dl], Wc[ci][:, kt, 0:384],
                                                   start=(kt == 0), stop=(kt == 7))
                        return ins
                    f.op(PE, rkeys + [("Wc", ci)], bkeys(bi), mmA)
                    si = rot_stg.next()
                    f.op(ACT, bkeys(bi), [("stg", si)],
                         lambda: nc.scalar.activation(stg[si][:, 0:384], bank(bi)[:, 0:384], AF.Copy))
                    f.op(ACT, bkeys(bi), [("V", g, pt)],
                         lambda: nc.scalar.activation(Vt[:, g, pt, :], bank(bi)[:, 256:384], AF.Copy))
                    qk = stg[si][:, 0:256].rearrange("p (h d) -> p h d", h=4)[:, :, 0:16]
                    A_ = stg[si][:, 512:576].rearrange("p (h d) -> p h d", h=4)
                    B_ = stg[si][:, 576:640].rearrange("p (h d) -> p h d", h=4)
                    cc = cs[:, pt, 0:16].unsqueeze(1).to_broadcast([128, 4, 16])
                    sn = cs[:, pt, 16:32].unsqueeze(1).to_broadcast([128, 4, 16])
                    k1, k2 = ("stgA", si), ("stgB", si)
                    f.op(DVE, [("stg", si), cskey], [k1], lambda: nc.vector.tensor_tensor(A_, qk, cc, ALU.mult))
                    f.op(DVE, [("stg", si), cskey], [k2], lambda: nc.vector.tensor_tensor(B_, qk, sn, ALU.mult))
                    f.op(DVE, [k1, k2], [("stg", si)],
                         lambda: nc.vector.tensor_tensor(qk[:, :, 0:8], A_[:, :, 0:8], B_[:, :, 8:16], ALU.subtract))
                    f.op(DVE, [k1, k2], [("stg", si)],
                         lambda: nc.vector.tensor_tensor(qk[:, :, 8:16], A_[:, :, 8:16], B_[:, :, 0:8], ALU.add))
                    keep = (mb == nblk - 1) if g < 2 else True
                    if keep:
                        if g == 0:
                            rows = kvp[0][l, :, :]
                        elif g == 1:
                            rows = kvp[1][l, r:512:4, :]
                        else:
                            rows = kvp[2][l, r:2048:16, :]
                        dst = rows.rearrange("t (kv h e) -> t kv h e", kv=2, h=4)[:, :, 2 * sp:2 * sp + 2, :]
                        src = stg[si][:, 128:384].rearrange("p (kv h e) -> p kv h e", kv=2, h=2)
                        f.dma(SP, dst, src, [("stg", si)], [], is_output=True)
                    st1[pt] = si

                def stage1b(pt):
                    si = st1.pop(pt)
                    qi = pt % 4
                    f.op(ACT, [("stg", si)], [("qkb", qi)], lambda: nc.scalar.activation(qkb[:, qi, :], stg[si][:, 0:256], AF.Copy))

                def stage2(pt):
                    qi = pt % 4
                    ti = rot_t.next()

                    def trA():
                        nc.tensor.transpose(psT[ti][:, 0:128], qkb[:, qi, 0:128], identb[:])
                        return nc.tensor.transpose(psT[ti][:, 128:256], qkb[:, qi, 128:256], identb[:])
                    f.op(PE, [("qkb", qi), "identb"], [("pt", ti)], trA)
                    f.op(ACT, [("pt", ti)], [("qkT", g, pt)],
                         lambda: nc.scalar.activation(qkT[:, g, :, pt * 128:(pt + 1) * 128],
                                                      psT[ti][:, 0:256].rearrange("p (c t) -> p c t", c=2), AF.Copy))
                LA = 3
                for i in range(16 + LA):
                    if i < 16:
                        stage1(i)
                    if 0 <= i - 1 < 16:
                        stage1b(i - 1)
                    if i - LA >= 0:
                        stage2(i - LA)
                bi = rot_b.next()

                def mmAs():
                    for kt in range(8):
                        ins = nc.tensor.matmul(bank(bi)[0:NB, 0:384], hTs[:, kt, :], Wc[ci][:, kt, 0:384],
                                               start=(kt == 0), stop=(kt == 7))
                    return ins
                f.op(PE, ["hTs", ("Wc", ci)], bkeys(bi), mmAs)
                c6 = sp * 3 + g
                f.op(ACT, bkeys(bi), [("zs", c6)],
                     lambda: nc.scalar.activation(zs[:, c6 * 384:(c6 + 1) * 384], bank(bi)[0:NB, 0:384], AF.Copy))

            if sp == 0:
                ci = load_chunk(l, OFF_GA, 256)
                for mt in range(2):
                    for nt in range(4):
                        bi = rot_b.next()

                        def mmB():
                            for kt in range(8):
                                ins = nc.tensor.matmul(bank(bi), Wc[ci][:, kt, mt * 128:(mt + 1) * 128],
                                                       hT[:, kt, nt * 512:(nt + 1) * 512], start=(kt == 0), stop=(kt == 7))
                            return ins
                        f.op(PE, hT_keys[nt * 4:nt * 4 + 4] + [("Wc", ci)], bkeys(bi), mmB)
                        f.op(ACT, bkeys(bi), [("gaT", mt, nt)],
                             lambda: nc.scalar.activation(gaT[:, mt, nt * 512:(nt + 1) * 512], bank(bi), AF.Silu))
                bi = rot_b.next()

                def mmBs():
                    for kt in range(8):
                        ins = nc.tensor.matmul(bank(bi)[0:NB, 0:256], hTs[:, kt, :], Wc[ci][:, kt, 0:256],
                                               start=(kt == 0), stop=(kt == 7))
                    return ins
                f.op(PE, ["hTs", ("Wc", ci)], bkeys(bi), mmBs)
                f.op(ACT, bkeys(bi), ["sga"], lambda: nc.scalar.activation(sga[:], bank(bi)[0:NB, 0:256], AF.Silu))

            if sp == 0:
                preload_chunk(l, OFF_A + 3 * 384, 384)
            else:
                preload_chunk(l, OFF_VB, 768)
            jobs = [(hh, g, pt) for hh in range(2) for g in range(3) for pt in range(16)]
            jstate = {}

            def c_stage1(j):
                hh, g, pt = jobs[j]
                nblk = (S // DIL[g]) // 128
                r, mb = divmod(pt, nblk)
                kbs = [pt - 1, pt] if mb > 0 else [pt]
                nk = len(kbs)
                sj = 2 + j % 4
                Sap = bank(sj)[:, 0:nk * 128]
                skey = bkeys(sj)

                def mmS():
                    for i, kb in enumerate(kbs):
                        ins = nc.tensor.matmul(Sap[:, i * 128:(i + 1) * 128],
                                               qkT[64 * hh:64 * hh + 64, g, 1, kb * 128:(kb + 1) * 128],
                                               qkT[64 * hh:64 * hh + 64, g, 0, pt * 128:(pt + 1) * 128],
                                               start=True, stop=True)
                    return ins
                f.op(PE, [("qkT", g, kb) for kb in kbs], skey, mmS)
                ei = j % 4
                Eap = Eb[:, ei, 0:nk * 128]
                f.op(ACT, skey, [("Eb", ei)], lambda: nc.scalar.activation(Eap, Sap, AF.Exp, scale=0.125))
                mk = maskb[:, 0:256] if nk == 2 else maskb[:, 128:256]
                if j % 3 == 2:
                    f.op(DVE, [("Eb", ei), "maskb"], [("Eb", ei)], lambda: nc.vector.tensor_tensor(Eap, Eap, mk, ALU.mult))
                else:
                    f.op(POOL, [("Eb", ei), "maskb"], [("Eb", ei)], lambda: nc.gpsimd.tensor_tensor(Eap, Eap, mk, ALU.mult))
                jstate[j] = (kbs, ei, Eap)

            def c_stage2(j):
                hh, g, pt = jobs[j]
                kbs, ei, Eap = jstate.pop(j)
                nb_ = 64 * hh
                db_ = 64 * (1 - hh)
                b4, qq = divmod(pt, 4)
                oi = (j // 4) % 2
                Ocols = slice(qq * 128, (qq + 1) * 128)
                nkk = len(kbs)

                def mmO():
                    for i, kb in enumerate(kbs):
                        nc.tensor.matmul(bank(oi)[nb_:nb_ + 64, Ocols], Vt[:, g, kb, 64 * hh:64 * hh + 64],
                                         Eap[:, i * 128:(i + 1) * 128], start=(i == 0), stop=(i == nkk - 1))
                    for i, kb in enumerate(kbs):
                        ins = nc.tensor.matmul(bank(oi)[db_:db_ + 64, Ocols], ones64[:],
                                               Eap[:, i * 128:(i + 1) * 128], start=(i == 0), stop=(i == nkk - 1))
                    return ins
                rd = [("Eb", ei), "ones64"] + [("V", g, kb) for kb in kbs]
                f._pre(PE, rd, bkeys(oi) if qq == 0 else [])
                ins = mmO()
                ev = PE.bump()
                ins.then_inc(ev[0], 1)
                f.ninstr["pe"] += 1
                f._post(ev, rd, bkeys(oi) if qq == 3 else [])
                if qq < 3:
                    return
                AK = [("accO", q) for q in range(4)]
                if g == 0:
                    f.op(DVE, bkeys(oi), [("accO", b4)],
                         lambda: nc.vector.tensor_copy(accO[:, b4 * 512:(b4 + 1) * 512], bank(oi)))
                elif g == 1:
                    dstv = accO[:, b4:2048:4]
                    f.op(DVE, bkeys(oi) + AK, AK, lambda: nc.vector.tensor_tensor(dstv, bank(oi), dstv, ALU.add))
                else:
                    dstv = accO.rearrange("p (m r) -> p r m", r=16)[:, 4 * b4:4 * b4 + 4, :]
                    srcv = bank(oi).rearrange("p (r m) -> p r m", r=4)
                    f.op(DVE, bkeys(oi) + AK, AK, lambda: nc.vector.tensor_tensor(dstv, srcv, dstv, ALU.add))
                if g == 2 and pt == 15:
                    f.op(ACT, AK, AK, lambda: nc.scalar.activation(accO[db_:db_ + 64, :], accO[db_:db_ + 64, :], AF.Ln))
                    for nt in range(4):
                        ri = nt % 2
                        cols = slice(nt * 512, (nt + 1) * 512)
                        f.op(ACT, AK, [("rec", ri)],
                             lambda: nc.scalar.activation(rec[nb_:nb_ + 64, ri, :], accO[db_:db_ + 64, cols], AF.Exp, scale=-1.0))
                        f.op(DVE, AK + [("rec", ri)], [("rec", ri)],
                             lambda: nc.vector.tensor_tensor(rec[nb_:nb_ + 64, ri, :], accO[nb_:nb_ + 64, cols],
                                                             rec[nb_:nb_ + 64, ri, :], ALU.mult))
                        f.op(DVE, [("rec", ri), ("gaT", sp, nt)], [("gaT", sp, nt)],
                             lambda: nc.vector.tensor_tensor(gaT[nb_:nb_ + 64, sp, cols], rec[nb_:nb_ + 64, ri, :],
                                                             gaT[nb_:nb_ + 64, sp, cols], ALU.mult))
            redA_ap = psT[0][:].bitcast(F32)
            redB_ap = psT[1][:].bitcast(F32)
            RED_KEYS = [("pt", 0), ("pt", 1)]
            zq = zs[:].rearrange("p (c t e) -> p c t e", c=6, t=3)

            def SA_pre():
                for t_ in range(2):
                    v16 = zs[:].rearrange("p (c t h d) -> p c t h d", c=6, t=3, h=2)[:, :, t_, :, 0:16]
                    A_ = sm[:, 0:192].rearrange("p (c h d) -> p c h d", c=6, h=2)
                    B_ = sm[:, 192:384].rearrange("p (c h d) -> p c h d", c=6, h=2)
                    cc = css[:, 0:16].unsqueeze(1).unsqueeze(1).to_broadcast([NB, 6, 2, 16])
                    sn = css[:, 16:32].unsqueeze(1).unsqueeze(1).to_broadcast([NB, 6, 2, 16])
                    f.op(DVE, ZK + ["css"], ["smA"], lambda: nc.vector.tensor_tensor(A_, v16, cc, ALU.mult))
                    f.op(DVE, ZK + ["css"], ["smB"], lambda: nc.vector.tensor_tensor(B_, v16, sn, ALU.mult))
                    f.op(DVE, ["smA", "smB"], ZK,
                         lambda: nc.vector.tensor_tensor(v16[:, :, :, 0:8], A_[:, :, :, 0:8], B_[:, :, :, 8:16], ALU.subtract))
                    f.op(DVE, ["smA", "smB"], ZK,
                         lambda: nc.vector.tensor_tensor(v16[:, :, :, 8:16], A_[:, :, :, 8:16], B_[:, :, :, 0:8], ALU.add))
                for g in range(3):
                    for kv in range(2):
                        src = zs[:].rearrange("p (s g t e) -> p g t s e", s=2, g=3, t=3)[:, g, 1 + kv, :, :]
                        dst = kvs[g][l][:, kv * 256:(kv + 1) * 256].rearrange("b (s e) -> b s e", s=2)
                        f.dma(SP, dst, src, ZK, [], is_output=True)

                f.dma(SP, qscr[l].rearrange("b (c e) -> b c e", c=6), zq[:, :, 0, :], ZK, [("qscr", l)], semkey=("qscr", l))
                f._pre(PE, [], RED_KEYS)

            def SA_S1(b):
                for g in range(3):
                    dl = DIL[g]
                    L = 128 * dl
                    srcKV = caches[g][l, b, 0:L:dl, :].rearrange("j (kv s e) -> j kv s e", kv=2, s=2)
                    dK = stg[0][:, 0:768].rearrange("p (s g e) -> p g s e", s=2, g=3)[:, g, :, :]
                    dV = stg[1][:, 0:768].rearrange("p (s g e) -> p g s e", s=2, g=3)[:, g, :, :]
                    f.dma(SP, dK, srcKV[:, 0, :, :], [], [("stg", 0)])
                    f.dma(SP, dV, srcKV[:, 1, :, :], [], [("stg", 1)])
                f.dma(SP, stg[3][:, 0:768], qscr[l, b].partition_broadcast(128), [("qscr", l)], [("stg", 3)])
                f.op(POOL, [("stg", 0), ("stg", 3)], [("stg", 2)],
                     lambda: nc.gpsimd.tensor_tensor(stg[2][:, 0:768], stg[0][:, 0:768], stg[3][:, 0:768], ALU.mult))
                sc = new_stat(12)
                f.op(DVE, [("stg", 2)], [("st", sc)],
                     lambda: nc.vector.tensor_reduce(stat[:, sc:sc + 12], stg[2][:, 0:768].rearrange("p (a d) -> p a d", d=64),
                                                     AX.X, ALU.add))
                f.op(ACT, [("st", sc)], [("stgE", 2)],
                     lambda: nc.scalar.activation(stg[2][:, 768:780], stat[:, sc:sc + 12], AF.Exp, scale=0.125))
                f.op(POOL, [("stg", 1), ("stgE", 2)], [("stg", 2)],
                     lambda: nc.gpsimd.tensor_tensor(
                         stg[2][:, 0:768].rearrange("p (a d) -> p a d", d=64), stg[1][:, 0:768].rearrange("p (a d) -> p a d", d=64),
                         stg[2][:, 768:780].unsqueeze(2).to_broadcast([128, 12, 64]), ALU.mult))

            def SA_S2(b):
                def mmr():
                    nc.tensor.matmul(redA_ap[0:NB, 0:512], oh[:, b, :], stg[2][:, 0:512], start=(b == 0), stop=(b == NB - 1))
                    return nc.tensor.matmul(redB_ap[0:NB, 0:268], oh[:, b, :], stg[2][:, 512:780], start=(b == 0), stop=(b == NB - 1))
                rd = [("stg", 2), ("stgE", 2), "oh"]
                f._pre(PE, rd, [])
                ins = mmr()
                ev = PE.bump()
                ins.then_inc(ev[0], 1)
                f.ninstr["pe"] += 1
                f._post(ev, rd, RED_KEYS if b == NB - 1 else [])

            def SA_post():
                f.op(DVE, [("pt", 0)], ["nd"], lambda: nc.vector.tensor_copy(nd[:, 0:512], redA_ap[0:NB, 0:512]))
                f.op(DVE, [("pt", 1)], ["nd"], lambda: nc.vector.tensor_copy(nd[:, 512:780], redB_ap[0:NB, 0:268]))
                sp_i = 4
                selfp = stg[sp_i][0:NB, 0:768]
                f.op(DVE, ZK, [("stg", sp_i)],
                     lambda: nc.vector.tensor_tensor(selfp.rearrange("p (c e) -> p c e", c=6), zq[:, :, 0, :], zq[:, :, 1, :], ALU.mult))
                ssc = new_stat(12)
                f.op(DVE, [("stg", sp_i)], [("st", ssc)],
                     lambda: nc.vector.tensor_reduce(stat[0:NB, ssc:ssc + 12], selfp.rearrange("p (a d) -> p a d", d=64), AX.X, ALU.add))
                ses = new_stat(12)
                f.op(ACT, [("st", ssc)], [("st", ses)],
                     lambda: nc.scalar.activation(stat[0:NB, ses:ses + 12], stat[0:NB, ssc:ssc + 12], AF.Exp, scale=0.125))
                f.op(DVE, ZK + [("st", ses)], [("stg", sp_i)],
                     lambda: nc.vector.tensor_tensor(
                         selfp.rearrange("p (c h d) -> p c h d", c=6, h=2),
                         zs[:].rearrange("p (c t h d) -> p c t h d", c=6, t=3, h=2)[:, :, 2, :, :],
                         stat[0:NB, ses:ses + 12].rearrange("p (c h) -> p c h", c=6).unsqueeze(3).to_broadcast([NB, 6, 2, 64]), ALU.mult))
                f.op(DVE, [("stg", sp_i), "nd"], ["nd"], lambda: nc.vector.tensor_tensor(nd[:, 0:768], nd[:, 0:768], selfp, ALU.add))
                f.op(DVE, [("st", ses), "nd"], ["nd"],
                     lambda: nc.vector.tensor_tensor(nd[:, 768:780], nd[:, 768:780], stat[0:NB, ses:ses + 12], ALU.add))
                ndv = nd[:, 0:768].rearrange("p (s g e) -> p s g e", s=2, g=3)
                dnv = nd[:, 768:780].rearrange("p (s g h) -> p s g h", s=2, g=3)
                nt_ = sm[:, 0:256].rearrange("p (s e) -> p s e", s=2)
                dt_ = sm[:, 256:260].rearrange("p (s h) -> p s h", s=2)
                f.op(DVE, ["nd", "smA", "smB"], ["smN"], lambda: nc.vector.tensor_tensor(nt_, ndv[:, :, 0, :], ndv[:, :, 1, :], ALU.add))
                f.op(DVE, ["nd", "smN"], ["smN"], lambda: nc.vector.tensor_tensor(nt_, nt_, ndv[:, :, 2, :], ALU.add))
                f.op(DVE, ["nd", "smB"], ["smD"], lambda: nc.vector.tensor_tensor(dt_, dnv[:, :, 0, :], dnv[:, :, 1, :], ALU.add))
                f.op(DVE, ["nd", "smD"], ["smD"], lambda: nc.vector.tensor_tensor(dt_, dt_, dnv[:, :, 2, :], ALU.add))
                f.op(DVE, ["smD"], ["smD"], lambda: nc.vector.reciprocal(sm[:, 256:260], sm[:, 256:260]))
                f.op(DVE, ["smN", "smD"], ["smN"],
                     lambda: nc.vector.tensor_tensor(sm[:, 0:256].rearrange("p (h d) -> p h d", h=4), sm[:, 0:256].rearrange("p (h d) -> p h d", h=4),
                                                     sm[:, 256:260].unsqueeze(2).to_broadcast([NB, 4, 64]), ALU.mult))
                f.op(DVE, ["smN", "sga"], ["hsb"], lambda: nc.vector.tensor_tensor(hsb[:, 0:256], sm[:, 0:256], sga[:], ALU.mult))
                ti = rot_t.next()

                def trga():
                    nc.tensor.transpose(psT[ti][:, 0:NB], hsb[:, 0:128], identb[0:NB, 0:NB])
                    return nc.tensor.transpose(psT[ti][:, NB:2 * NB], hsb[:, 128:256], identb[0:NB, 0:NB])
                f.op(PE, ["hsb", "identb"], [("pt", ti)], trga)
                f.op(DVE, [("pt", ti)], ["gaTs"],
                     lambda: nc.vector.tensor_copy(gaTs[:], psT[ti][:, 0:2 * NB].rearrange("p (k t) -> p k t", k=2)))

            sa_sched = {}
            if sp == 1:
                sa_sched = {2: [SA_pre], 8: [lambda: SA_S1(0)], 24: [lambda: SA_S2(0), lambda: SA_S1(1)],
                            40: [lambda: SA_S2(1), lambda: SA_S1(2)], 56: [lambda: SA_S2(2), lambda: SA_S1(3)],
                            72: [lambda: SA_S2(3)], 80: [SA_post]}
            LA = 3
            for i in range(len(jobs) + LA):
                if i < len(jobs):
                    c_stage1(i)
                if i - LA >= 0:
                    c_stage2(i - LA)
                for fn_ in sa_sched.get(i, []):
                    fn_()


        pbD_keys = [("qkT", g, pt) for g in range(3) for pt in range(16)] + [("V", g, pt) for g in range(3) for pt in range(16)] + \
                   [("accO", q) for q in range(4)] + [("rec", i) for i in range(2)] + [("Eb", i) for i in range(4)] + \
                   [("qkb", i) for i in range(4)] + [("cs", i) for i in range(2)]
        f.sync_on(ACT, pbD_keys + ZK)
        f.sync_on(DVE, pbD_keys + ZK)
        f.op(DVE, ["wsT32", "mask32"], ["wsT"],
             lambda: nc.vector.tensor_tensor(wsT[:], wsT32[:], mask32[:].unsqueeze(1).to_broadcast([128, 4, 128]), ALU.mult))
        ci = load_chunk(l, OFF_VB, 768)
        preload_chunk(l, OFF_U, 768)

        def ln_chain(P, s1, s2, n=1):
            mean, msq, var = new_stat(n), new_stat(n), new_stat(n)
            f.op(DVE, [("st", s1)], [("st", mean)],
                 lambda: nc.vector.tensor_scalar(stat[0:P, mean:mean + n], stat[0:P, s1:s1 + n], 1.0 / 768, None, ALU.mult))
            f.op(DVE, [("st", mean)], [("st", msq)],
                 lambda: nc.vector.tensor_tensor(stat[0:P, msq:msq + n], stat[0:P, mean:mean + n], stat[0:P, mean:mean + n], ALU.mult))
            f.op(DVE, [("st", s2), ("st", msq)], [("st", var)],
                 lambda: nc.vector.scalar_tensor_tensor(stat[0:P, var:var + n], stat[0:P, s2:s2 + n], 1.0 / 768, stat[0:P, msq:msq + n],
                                                        ALU.mult, ALU.subtract))
            rs = rstd_from_ss(var, 1.0, P, n)
            nmr = new_stat(n)
            f.op(DVE, [("st", mean), ("st", rs)], [("st", nmr)],
                 lambda: nc.vector.scalar_tensor_tensor(stat[0:P, nmr:nmr + n], stat[0:P, mean:mean + n], -1.0, stat[0:P, rs:rs + n],
                                                        ALU.mult, ALU.mult))
            return rs, nmr

        def vn_tile(P, src_psum_ap, src_keys, g32, g32keys, out_ap, out_keys):
            s1, s2 = acc_stat(), acc_stat()
            f.op(ACT, src_keys + ["stat"], g32keys + [("st", s1)],
                 lambda: nc.scalar.activation(g32, src_psum_ap, AF.Gelu_apprx_tanh, accum_out=stat[0:P, s1:s1 + 1]))
            f.op(ACT, g32keys + ["stat"], [("st", s2), "junk"],
                 lambda: nc.scalar.activation(junk[0:P, 0:768], g32, AF.Square, accum_out=stat[0:P, s2:s2 + 1]))
            rs, nmr = ln_chain(P, s1, s2)
            f.op(DVE, g32keys + [("st", rs), ("st", nmr)], g32keys,
                 lambda: nc.vector.tensor_scalar(g32, g32, stat[0:P, rs:rs + 1], stat[0:P, nmr:nmr + 1], ALU.mult, ALU.add))
            f.op(DVE, g32keys + ["gam"], g32keys, lambda: nc.vector.tensor_tensor(g32, g32, gam[0:P, :], ALU.mult))
            f.op(DVE, g32keys + ["beta"], out_keys, lambda: nc.vector.tensor_tensor(out_ap, g32, beta[0:P, :], ALU.add))

        def mm768(P, lhs_fn, lhs_keys, ci):
            di = rot_d.next()

            def mm():
                for kt in range(8):
                    nc.tensor.matmul(psA[di][0:P, 0:512], lhs_fn(kt), Wc[ci][:, kt, 0:512], start=(kt == 0), stop=(kt == 7))
                    ins = nc.tensor.matmul(psA[di][0:P, 512:768], lhs_fn(kt), Wc[ci][:, kt, 512:768], start=(kt == 0), stop=(kt == 7))
                return ins
            dk = bkeys(2 * di) + bkeys(2 * di + 1)
            f.op(PE, lhs_keys + [("Wc", ci)], dk, mm)
            return di, dk

        Z0, Z1, Z2 = [("zs", 0), ("zs", 1)], [("zs", 2), ("zs", 3)], [("zs", 4), ("zs", 5)]
        di, dk = mm768(NB, lambda kt: hTs[:, kt, :], ["hTs"], ci)
        vn_tile(NB, psA[di][0:NB, 0:768], dk, zs[:, 0:768], Z0, zs[:, 0:768], Z0)
        f.dma(SP, gv_s[l], zs[:, 0:768], Z0, [], is_output=True)
        for g4 in range(4):
            f.op(DVE, Z0 + ["ws00", "bs0"], Z0,
                 lambda: nc.vector.tensor_scalar(zs[:, g4 * 192:(g4 + 1) * 192], zs[:, g4 * 192:(g4 + 1) * 192],
                                                 ws00[:, l * 4 + g4:l * 4 + g4 + 1], bs0[:, l * 4 + g4:l * 4 + g4 + 1],
                                                 ALU.mult, ALU.add))
        stgX = PB[:, 28672:30720].bitcast(F32)
        stgD = [(stg[i][:, 0:768], ("stg", i)) for i in range(5)] + [(stgX[:, 0:768], "stgX")]
        rotD = Rot(6)
        for g4 in range(4):
            s1b, s2b = acc_stat(4), acc_stat(4)
            bufs = []
            for c in range(4):
                tt = g4 * 4 + c
                di, dk = mm768(128, lambda kt: hT[:, kt, tt * 128:(tt + 1) * 128], [("hT", tt)], ci)
                g32, gkey = stgD[rotD.next()]
                f.op(ACT, dk + ["stat"], [gkey, ("st", s1b)],
                     lambda: nc.scalar.activation(g32, psA[di][:, 0:768], AF.Gelu_apprx_tanh, accum_out=stat[:, s1b + c:s1b + c + 1]))
                f.op(ACT, [gkey, "stat"], [("st", s2b), "junk"],
                     lambda: nc.scalar.activation(junk[:, 0:768], g32, AF.Square, accum_out=stat[:, s2b + c:s2b + c + 1]))
                bufs.append((g32, gkey, tt))
            rs4, nmr4 = ln_chain(128, s1b, s2b, 4)
            for c, (g32, gkey, tt) in enumerate(bufs):
                f.op(DVE, [gkey, ("st", rs4), ("st", nmr4)], [gkey],
                     lambda: nc.vector.tensor_scalar(g32, g32, stat[:, rs4 + c:rs4 + c + 1], stat[:, nmr4 + c:nmr4 + c + 1], ALU.mult, ALU.add))
                f.op(DVE, [gkey, "gam"], [gkey], lambda: nc.vector.tensor_tensor(g32, g32, gam[:, :], ALU.mult))
                f.op(POOL, [gkey, "beta"], [("vn", tt)], lambda: nc.gpsimd.tensor_tensor(vn[:, tt, :], g32, beta[:, :], ALU.add))
        ci = load_chunk(l, OFF_U, 768)
        for mt in range(6):
            for nt in range(4):
                bi = rot_b.next()

                def mmD2():
                    for kt in range(8):
                        ins = nc.tensor.matmul(bank(bi), Wc[ci][:, kt, mt * 128:(mt + 1) * 128], hT[:, kt, nt * 512:(nt + 1) * 512],
                                               start=(kt == 0), stop=(kt == 7))
                    return ins
                f.op(PE, hT_keys[nt * 4:nt * 4 + 4] + [("Wc", ci)], bkeys(bi), mmD2)
                f.op(ACT, bkeys(bi), [("ugT", mt, nt)],
                     lambda: nc.scalar.activation(ugT[:, mt, nt * 512:(nt + 1) * 512], bank(bi), AF.Gelu_apprx_tanh))
        di, dk = mm768(NB, lambda kt: hTs[:, kt, :], ["hTs"], ci)
        f.op(ACT, dk, Z1, lambda: nc.scalar.activation(zs[:, 768:1536], psA[di][0:NB, 0:768], AF.Gelu_apprx_tanh))
        f.op(DVE, Z0 + Z1, Z0, lambda: nc.vector.tensor_tensor(zs[:, 0:768], zs[:, 0:768], zs[:, 768:1536], ALU.mult))
        preload_chunk(l, OFF_GB, 768)
        for mt in range(6):
            for tg in range(4):
                bi = rot_b.next()

                def mmD3():
                    for c in range(4):
                        tt = tg * 4 + c
                        oc = slice(c * 128, (c + 1) * 128)
                        if mt in (1, 4):
                            ga_ = 0 if mt == 1 else 2
                            nc.tensor.matmul(bank(bi)[0:64, oc], vn[:, tt, mt * 128:mt * 128 + 64], wsT[:, ga_, :], start=True, stop=True)
                            ins = nc.tensor.matmul(bank(bi)[64:128, oc], vn[:, tt, mt * 128 + 64:mt * 128 + 128], wsT[:, ga_ + 1, :],
                                                   start=True, stop=True)
                        else:
                            g_ = (mt * 128) // 192
                            ins = nc.tensor.matmul(bank(bi)[:, oc], vn[:, tt, mt * 128:(mt + 1) * 128], wsT[:, g_, :], start=True, stop=True)
                    return ins
                f.op(PE, [("vn", tg * 4 + c) for c in range(4)] + ["wsT"], bkeys(bi), mmD3)
                si = rot_stg.next()
                f.op(DVE, bkeys(bi) + ["bsT"], [("stg", si)],
                     lambda: nc.vector.tensor_tensor(
                         stg[si][:, 0:512].rearrange("p (c i) -> p c i", c=4), bank(bi).rearrange("p (c i) -> p c i", c=4),
                         bsT[:, mt, :].unsqueeze(1).to_broadcast([128, 4, 128]), ALU.add))
                f.op(POOL, [("stg", si), ("ugT", mt, tg)], [("ugT", mt, tg)],
                     lambda: nc.gpsimd.tensor_tensor(ugT[:, mt, tg * 512:(tg + 1) * 512], stg[si][:, 0:512],
                                                     ugT[:, mt, tg * 512:(tg + 1) * 512], ALU.mult))
        ci = load_chunk(l, OFF_GB, 768)
        k_tb = 0
        for mt in range(6):
            for nt in range(4):
                bi = rot_b.next()

                def mmD4():
                    for kt in range(8):
                        ins = nc.tensor.matmul(bank(bi), Wc[ci][:, kt, mt * 128:(mt + 1) * 128], hT[:, kt, nt * 512:(nt + 1) * 512],
                                               start=(kt == 0), stop=(kt == 7))
                    return ins
                f.op(PE, hT_keys[nt * 4:nt * 4 + 4] + [("Wc", ci)], bkeys(bi), mmD4)
                tb = k_tb % 2
                k_tb += 1
                f.op(ACT, bkeys(bi), [("tmpb", tb)], lambda: nc.scalar.activation(tmpb[:, tb, :], bank(bi), AF.Silu))
                f.op(DVE, [("tmpb", tb), ("ugT", mt, nt)], [("ugT", mt, nt)],
                     lambda: nc.vector.tensor_tensor(ugT[:, mt, nt * 512:(nt + 1) * 512], tmpb[:, tb, :],
                                                     ugT[:, mt, nt * 512:(nt + 1) * 512], ALU.mult))
        di, dk = mm768(NB, lambda kt: hTs[:, kt, :], ["hTs"], ci)
        f.op(ACT, dk, Z2, lambda: nc.scalar.activation(zs[:, 1536:2304], psA[di][0:NB, 0:768], AF.Silu))
        f.op(DVE, Z0 + Z2, ["hsb"], lambda: nc.vector.tensor_tensor(hsb[:, 0:768], zs[:, 0:768], zs[:, 1536:2304], ALU.mult))
        ti = rot_t.next()

        def trug():
            for k in range(6):
                ins = nc.tensor.transpose(psT[ti][:, k * NB:(k + 1) * NB], hsb[:, k * 128:(k + 1) * 128], identb[0:NB, 0:NB])
            return ins
        f.op(PE, ["hsb", "identb"], [("pt", ti)], trug)
        f.op(DVE, [("pt", ti)], ["ugTs"],
             lambda: nc.vector.tensor_copy(ugTs[:], psT[ti][:, 0:6 * NB].rearrange("p (k t) -> p k t", k=6)))

        pbE_keys = [("vn", t) for t in range(16)] + [("tmpb", i) for i in range(2)] + ["stgX"]
        f.sync_on(DVE, pbE_keys)
        f.sync_on(ACT, pbE_keys)
        f.dma(SP, gpost[:], gpost_d[l].partition_broadcast(128), [], ["gpost"])
        f.dma(POOL, wpa[:], wpa_d[l], [], ["wpa"])
        wpbi = rot_wc.next()
        wpb = Wc[wpbi][:].rearrange("p k c -> p (k c)")[:, 0:6144].rearrange("p (k c) -> p k c", k=6)
        f.dma(POOL, wpb, wpb_d[l], [], [("Wc", wpbi)])
        k_sg = 0
        e_bufs = []
        for qd in range(4):
            ci = load_chunk(l, OFF_E + qd * 512, 512, avoid=(wpbi,))
            e_bufs.append(ci)
            for mfl in range(2):
                mf = qd * 2 + mfl
                for nt in range(5):
                    smp = nt == 4
                    ncol = NB if smp else 512
                    rh = hTs[:] if smp else hT[:, :, nt * 512:(nt + 1) * 512]
                    rga = gaTs[:] if smp else gaT[:, :, nt * 512:(nt + 1) * 512]
                    rug = ugTs[:] if smp else ugT[:, :, nt * 512:(nt + 1) * 512]
                    hk = ["hTs"] if smp else hT_keys[nt * 4:nt * 4 + 4]
                    gk = ["gaTs"] if smp else [("gaT", 0, nt), ("gaT", 1, nt)]
                    uk = ["ugTs"] if smp else [("ugT", a, nt) for a in range(6)]
                    bla, blb, bba, bbb = rot_b.next(), rot_b.next(), rot_b.next(), rot_b.next()

                    def mmL(bo, coff):
                        for kt in range(8):
                            ins = nc.tensor.matmul(bank(bo)[:, 0:ncol], Wc[ci][:, kt, coff + mfl * 128:coff + mfl * 128 + 128], rh[:, kt, :],
                                                   start=(kt == 0), stop=(kt == 7))
                        return ins
                    f.op(PE, hk + [("Wc", ci)], bkeys(bla), lambda: mmL(bla, 0))
                    f.op(PE, hk + [("Wc", ci)], bkeys(blb), lambda: mmL(blb, 256))

                    def mmBa():
                        for kt in range(2):
                            ins = nc.tensor.matmul(bank(bba)[:, 0:ncol], wpa[:, kt, mf * 128:(mf + 1) * 128], rga[:, kt, :],
                                                   start=(kt == 0), stop=(kt == 1))
                        return ins
                    f.op(PE, gk + ["wpa"], bkeys(bba), mmBa)

                    def mmBb():
                        for kt in range(6):
                            ins = nc.tensor.matmul(bank(bbb)[:, 0:ncol], wpb[:, kt, mf * 128:(mf + 1) * 128], rug[:, kt, :],
                                                   start=(kt == 0), stop=(kt == 5))
                        return ins
                    f.op(PE, uk + [("Wc", wpbi)], bkeys(bbb), mmBb)
                    sg = k_sg % 2
                    k_sg += 1
                    sa_ = hb[sg][:, 0:ncol]
                    sb_ = hb[sg][:, 512:512 + ncol]
                    f.op(ACT, bkeys(bla) + ["bm"], [("hbA", sg), ("hb", sg)],
                         lambda: nc.scalar.activation(sa_, bank(bla)[:, 0:ncol], AF.Sigmoid, bias=bm[:, l, mf:mf + 1], scale=1.0))
                    f.op(ACT, bkeys(blb) + ["bm"], [("hbB", sg)],
                         lambda: nc.scalar.activation(sb_, bank(blb)[:, 0:ncol], AF.Sigmoid, bias=bm[:, l, 8 + mf:8 + mf + 1], scale=1.0))
                    si = rot_stg.next()
                    t1 = stg[si][:, 0:ncol]
                    t2 = stg[si][:, 512:512 + ncol]
                    f.op(DVE, bkeys(bba) + [("hbA", sg)], [("stgA", si), ("stg", si)],
                         lambda: nc.vector.tensor_tensor(t1, bank(bba)[:, 0:ncol], sa_, ALU.mult))
                    f.op(DVE, bkeys(bbb) + [("hbB", sg)], [("stgB", si), ("hb", sg)],
                         lambda: nc.vector.tensor_tensor(t2, bank(bbb)[:, 0:ncol], sb_, ALU.mult))
                    mdst = mTs[:, mf, :] if smp else mT[:, mf, nt * 512:(nt + 1) * 512]
                    mkey = "mTs" if smp else ("mT", mf)
                    f.op(DVE, [("stgA", si), ("stgB", si)], [mkey, ("stg", si)],
                         lambda: nc.vector.tensor_tensor(mdst, t1, t2, ALU.add))

        wo = [e_bufs[2], wpbi]
        for h2 in range(2):
            f.dma(POOL, Wc[wo[h2]][:, :, 0:512], wout_d[l, :, :, h2 * 512:(h2 + 1) * 512], [], [("Wc", wo[h2])])

        preload_chunk(l + 1, OFF_A, 384, avoid=(wo[0], wo[1]))

        def phaseF1(P, lhs_fn, lhs_keys, x_ap, xkey, tmp_ap, tmp_key, out_ap, out_key):
            di = rot_d.next()

            def mmF():
                for kt in range(8):
                    nc.tensor.matmul(psA[di][0:P, 0:512], lhs_fn(kt), Wc[wo[0]][:, kt, 0:512], start=(kt == 0), stop=(kt == 7))
                for kt in range(8):
                    ins = nc.tensor.matmul(psA[di][0:P, 512:1024], lhs_fn(kt), Wc[wo[1]][:, kt, 0:512], start=(kt == 0), stop=(kt == 7))
                return ins
            dk = bkeys(2 * di) + bkeys(2 * di + 1)
            f.op(PE, lhs_keys + [("Wc", wo[0]), ("Wc", wo[1])], dk, mmF)
            ss = acc_stat()
            f.op(ACT, dk + ["stat"], [("st", ss), "junk"],
                 lambda: nc.scalar.activation(junk[0:P, :], psA[di][0:P, :], AF.Square, accum_out=stat[0:P, ss:ss + 1]))
            rs = rstd_from_ss(ss, 1.0 / D, P)
            f.op(DVE, dk + [("st", rs), "gpost"], [tmp_key],
                 lambda: nc.vector.scalar_tensor_tensor(tmp_ap, psA[di][0:P, :], stat[0:P, rs:rs + 1], gpost[0:P, :], ALU.mult, ALU.mult))
            if P == 128:
                f.op(POOL, [tmp_key, xkey], [out_key], lambda: nc.gpsimd.tensor_tensor(out_ap, tmp_ap, x_ap, ALU.add))
            else:
                f.op(DVE, [tmp_key, xkey], [out_key], lambda: nc.vector.tensor_tensor(out_ap, tmp_ap, x_ap, ALU.add))
        phaseF = phaseF1

        xsrc = xp if l == 0 else xscr
        GA_KEYS = [("gaT", a, n) for a in range(2) for n in range(4)]
        f.sync_on(ACT, GA_KEYS + UG_KEYS)

        def load_x(tt):
            xi = rot_stg.next()
            f.dma(SP, stg[xi][:], xsrc[tt * 128:(tt + 1) * 128, :], [("xscr", tt)] if l > 0 else [], [("stg", xi)])
            return xi
        xnext = load_x(0)
        st_f1, st_f2 = {}, {}

        def F1(tt):
            nonlocal_x = st_f1.pop(("x", tt))
            oi = rot_stg.next()
            phaseF(128, lambda kt: mT[:, kt, tt * 128:(tt + 1) * 128], MT_KEYS,
                   stg[nonlocal_x][:], ("stg", nonlocal_x), stg[oi][:], ("stg", oi), stg[oi][:], ("stg", oi))
            if last:
                f.dma(SP, y_p[tt * 128:(tt + 1) * 128, :], stg[oi][:], [("stg", oi)], [], is_output=True)
            else:
                f.dma(SP, xscr[tt * 128:(tt + 1) * 128, :], stg[oi][:], [("stg", oi)], [("xscr", tt)], semkey=("xscr", tt))
            st_f1[tt] = oi

        def F2(tt):
            oi = st_f1.pop(tt)
            st_f2[tt] = prenorm_a(stg[oi][:], ("stg", oi), 128)

        def F3(tt):
            prenorm_b(l + 1, st_f2.pop(tt), 128, hT[:, :, tt * 128:(tt + 1) * 128], ("hT", tt))

        for i in range(16 + 3):
            if i < 16:
                st_f1[("x", i)] = xnext
                if i + 1 < 16:
                    xnext = load_x(i + 1)
                F1(i)
            if not last:
                if 0 <= i - 1 < 16:
                    F2(i - 1)
                if 0 <= i - 3 < 16:
                    F3(i - 3)
        oi = rot_stg.next()
        phaseF(NB, lambda kt: mTs[:, kt, :], ["mTs"], xs_t[:], "xs_t", stg[oi][0:NB, :], ("stg", oi), xs_t[:], "xs_t")
        if last:
            f.dma(SP, y_s, xs_t[:], ["xs_t"], [], is_output=True)
        else:
            prenorm(l + 1, xs_t[:], "xs_t", NB, hTs[:], "hTs")

    f.finish()
    f.close()
    return nc, f


_PROG = {}


def _col_perm():
    cols = []
    for sp in range(2):
        for g in range(3):
            for base in (0, 768, 1536):
                cols.extend(range(base + g * 256 + sp * 128, base + g * 256 + sp * 128 + 128))
    cols.extend(range(2304, 2560))
    cols.extend(range(3328, 4096))
    cols.extend(range(2560, 3328))
    cols.extend(range(4096, 4864))
    for qd in range(4):
        cols.extend(range(4864 + qd * 256, 4864 + qd * 256 + 256))
        cols.extend(range(5888 + qd * 256, 5888 + qd * 256 + 256))
    assert len(cols) == N_IN
    return np.asarray(cols)


def _rope_table(pos):
    half = 8
    inv_freq = np.power(np.float32(500000.0), -np.arange(half, dtype=np.float32) / np.float32(half)).astype(np.float32)
    ang = pos.astype(np.float32)[..., None] * inv_freq
    c = np.cos(ang).astype(np.float32)
    s = np.sin(ang).astype(np.float32)
    return np.concatenate([c, c, s, s], axis=-1)


def _constants():
    ident = np.eye(128, dtype=np.float32)
    j = np.arange(128)[:, None]
    i = np.arange(128)[None, :]
    mask = np.concatenate([(j >= i), (j <= i)], axis=1).astype(np.float32)
    maskbias = ((mask - 1.0) * 30000.0).astype(np.float32)
    cs = np.zeros((3, 128, 16, 32), np.float32)
    for g, dl in enumerate(DIL):
        nblk = (S // dl) // 128
        for pt in range(16):
            r, mb = divmod(pt, nblk)
            pos = r + dl * (mb * 128 + np.arange(128))
            cs[g, :, pt, :] = _rope_table(pos)
    css = np.repeat(_rope_table(np.asarray([8192]))[0][None], NB, axis=0)
    sel = np.zeros((NB, NB, 128), np.float32)
    oh = np.zeros((128, NB, NB), np.float32)
    for b in range(NB):
        sel[b, b, :] = 1.0
        oh[:, b, b] = 1.0
    return dict(ident=ident, mask=mask, maskbias=maskbias, cs=cs, css=css.astype(np.float32), sel=sel, oh=oh)


def kernel(x_prompt, x_sample, cache_kv_w128, cache_kv_w512, cache_kv_w2048, norm_pre, w_in, b_merge,
           v_norm_g, v_norm_b, w_spatial, b_spatial, w_proj_a, w_proj_b, w_out, norm_post, _nlayers=DEPTH, _cores=8, _trace=False):
    f32 = np.float32
    if _nlayers not in _PROG:
        _PROG[_nlayers] = build_program(_nlayers)[0]
    nc = _PROG[_nlayers]
    ncores = _cores
    perm = _col_perm()
    w_in_l = np.ascontiguousarray(np.asarray(w_in, f32)[:, :, perm].reshape(DEPTH, 8, 128, N_IN).transpose(0, 2, 1, 3))
    wpa = np.ascontiguousarray(np.asarray(w_proj_a, f32).reshape(DEPTH, 2, 128, D).transpose(0, 2, 1, 3))
    wpb = np.ascontiguousarray(np.asarray(w_proj_b, f32).reshape(DEPTH, 6, 128, D).transpose(0, 2, 1, 3))
    wout = np.ascontiguousarray(np.asarray(w_out, f32).reshape(DEPTH, 8, 128, D).transpose(0, 2, 1, 3))
    gpreT = np.ascontiguousarray(np.asarray(norm_pre, f32).reshape(DEPTH, 8, 128).transpose(2, 0, 1))
    bm = np.ascontiguousarray(np.asarray(b_merge, f32).reshape(DEPTH, 16, 128).transpose(2, 0, 1))
    wsT = np.ascontiguousarray(np.asarray(w_spatial, f32).transpose(0, 3, 1, 2))
    grp = (np.arange(768) // 192).reshape(6, 128)
    bsT = np.ascontiguousarray(np.asarray(b_spatial, f32)[:, grp, :].transpose(0, 2, 1, 3))
    ws00 = np.ascontiguousarray(np.asarray(w_spatial, f32)[:, :, 0, 0].reshape(-1))
    bs0 = np.ascontiguousarray(np.asarray(b_spatial, f32)[:, :, 0].reshape(-1))
    consts = _constants()
    shared = dict(w_in=w_in_l, wpa=wpa, wpb=wpb, wout=wout, gpreT=gpreT, gpost=np.asarray(norm_post, f32),
                  bm=bm, gam=np.asarray(v_norm_g, f32), beta=np.asarray(v_norm_b, f32), wsT=wsT, bsT=bsT,
                  ws00=ws00, bs0=bs0, **consts)
    xp = np.asarray(x_prompt, f32)
    xs = np.asarray(x_sample, f32)
    c128 = np.asarray(cache_kv_w128, f32).reshape(DEPTH, 32, 128, 512)
    c512 = np.asarray(cache_kv_w512, f32).reshape(DEPTH, 32, 512, 512)
    c2048 = np.asarray(cache_kv_w2048, f32).reshape(DEPTH, 32, 2048, 512)
    in_maps = []
    for c in range(ncores):
        m = dict(shared)
        m["xp"] = np.ascontiguousarray(xp[c])
        m["xs"] = np.ascontiguousarray(xs[NB * c:NB * c + NB, 0, :])
        m["c128"] = np.ascontiguousarray(c128[:, NB * c:NB * c + NB])
        m["c512"] = np.ascontiguousarray(c512[:, NB * c:NB * c + NB])
        m["c2048"] = np.ascontiguousarray(c2048[:, NB * c:NB * c + NB])
        in_maps.append(m)
    res = run_bass_kernel_spmd(nc, in_maps, core_ids=list(range(ncores)), **({"trace": True} if _trace else {}))
    if _trace:
        print("EXEC_TIME_NS", res.exec_time_ns)
    R = res.results
    y_prompt = np.stack([R[c]["y_p"] for c in range(ncores)], axis=0).astype(f32)
    y_sample = np.concatenate([R[c]["y_s"] for c in range(ncores)], axis=0).reshape(NB * ncores, 1, D).astype(f32)
    outs = [y_prompt, y_sample]
    for name, keep in (("kv128p", 128), ("kv512p", 512), ("kv2048p", 2048)):
        a = np.stack([R[c][name] for c in range(ncores)], axis=1)
        outs.append(a.reshape(DEPTH, ncores, keep, 2, 4, 64).astype(f32))
    for name in ("kv128s", "kv512s", "kv2048s"):
        a = np.concatenate([R[c][name] for c in range(ncores)], axis=1)
        outs.append(a.reshape(DEPTH, NB * ncores, 1, 2, 4, 64).astype(f32))
    a = np.concatenate([R[c]["gv_s"] for c in range(ncores)], axis=1)
    outs.append(a.reshape(DEPTH, NB * ncores, 1, 768).astype(f32))
    return tuple(outs)
```

```python
import contextlib
import math
import numpy as np
import concourse.bass as bass
import concourse.mybir as mybir
from concourse.bass_utils import run_bass_kernel_spmd

F32 = mybir.dt.float32
BF16 = mybir.dt.bfloat16
AF = mybir.ActivationFunctionType
ALU = mybir.AluOpType
AX = mybir.AxisListType

DEPTH = 4
S = 2048
D = 1024
NB = 4
N_IN = 6912
EPS = 1e-6
DIL = (1, 4, 16)
SEM_LIMIT = 16000

OFF_A = 0
OFF_GA = 2304
OFF_VB = 2560
OFF_U = 3328
OFF_GB = 4096
OFF_E = 4864


class _Eng:
    def __init__(self, fw, name, h):
        self.fw, self.name, self.h = fw, name, h
        self.sem = None
        self.val = 0
        self.nsem = 0
        self.waited = {}

    def bump(self):
        if self.sem is None or self.val + 1 > SEM_LIMIT:
            self.sem = self.fw.new_sem(f"e_{self.name}_{self.nsem}")
            self.nsem += 1
            self.val = 0
        self.val += 1
        return (self.sem, self.val, self.name)


class _DmaSem:
    def __init__(self, name):
        self.name = name
        self.sem = None
        self.val = 0
        self.nsem = 0


class _Res:
    __slots__ = ("w", "r")

    def __init__(self):
        self.w = None
        self.r = {}


class FW:
    def __init__(self, nc):
        self.nc = nc
        self.stack = contextlib.ExitStack()
        self.res = {}
        self.dsem = {}
        self.nsems = 0
        self.out_events = {}
        self.pe = _Eng(self, "pe", nc.tensor)
        self.act = _Eng(self, "act", nc.scalar)
        self.dve = _Eng(self, "dve", nc.vector)
        self.pool = _Eng(self, "pool", nc.gpsimd)
        self.sp = _Eng(self, "sp", nc.sync)
        self.ninstr = {"pe": 0, "act": 0, "dve": 0, "pool": 0, "sp": 0, "wait": 0}

    def new_sem(self, name):
        self.nsems += 1
        return self.stack.enter_context(self.nc.semaphore(name))

    def sbuf(self, name, shape, dt):
        return self.stack.enter_context(self.nc.sbuf_tensor("sb_" + name, shape, dt))

    def psum(self, name, shape, dt):
        return self.stack.enter_context(self.nc.psum_tensor("pp_" + name, shape, dt))

    def _r(self, k):
        r = self.res.get(k)
        if r is None:
            r = self.res[k] = _Res()
        return r

    def _need(self, eng, ev, kind):
        sem, val, src = ev
        if src == eng.name and (kind != "RAW" or src == "pe"):
            return
        key = id(sem)
        if eng.waited.get(key, 0) >= val:
            return
        eng.h.wait_ge(sem, val)
        self.ninstr["wait"] += 1
        eng.waited[key] = val

    def _pre(self, eng, reads, writes):
        for k in reads:
            r = self._r(k)
            if r.w is not None:
                self._need(eng, r.w, "RAW")
        for k in writes:
            r = self._r(k)
            if r.w is not None:
                self._need(eng, r.w, "WAW")
            for e in r.r.values():
                self._need(eng, e, "WAR")

    def _post(self, ev, reads, writes):
        for k in reads:
            self._r(k).r[id(ev[0])] = ev
        for k in writes:
            r = self._r(k)
            r.w = ev
            r.r = {}

    def sync_on(self, eng, keys):
        for k in keys:
            r = self.res.get(k)
            if r is None:
                continue
            if r.w is not None:
                self._need(eng, r.w, "RAW")
            for e in r.r.values():
                self._need(eng, e, "RAW")

    def op(self, eng, reads, writes, fn):
        self._pre(eng, reads, writes)
        ins = fn()
        ev = eng.bump()
        ins.then_inc(ev[0], 1)
        self.ninstr[eng.name] += 1
        self._post(ev, reads, writes)
        return ev

    def dma(self, q, out, in_, reads, writes, semkey=None, is_output=False):
        if semkey is None:
            semkey = writes[0] if writes else reads[0]
        ds = self.dsem.get(semkey)
        if ds is None:
            ds = self.dsem[semkey] = _DmaSem("d%d" % len(self.dsem))
        self._pre(q, reads, writes)
        if ds.sem is None or ds.val + 16 > SEM_LIMIT:
            if ds.sem is not None:
                self._need(q, (ds.sem, ds.val, "dma"), "RAW")
            ds.sem = self.new_sem(f"{ds.name}_{ds.nsem}")
            ds.nsem += 1
            ds.val = 0
        ins = q.h.dma_start(out=out, in_=in_)
        ds.val += 16
        ev = (ds.sem, ds.val, "dma")
        ins.then_inc(ds.sem, 16)
        self.ninstr[q.name] += 1
        self._post(ev, reads, writes)
        if is_output:
            self.out_events[id(ds.sem)] = ev
        return ev

    def finish(self):
        for ev in list(self.out_events.values()):
            self._need(self.sp, ev, "RAW")
        for e in (self.pe, self.act, self.dve, self.pool):
            if e.sem is not None:
                self._need(self.sp, (e.sem, e.val, e.name), "RAW")

    def close(self):
        self.stack.close()


class Rot:
    def __init__(self, n):
        self.n, self.i = n, -1

    def next(self):
        self.i = (self.i + 1) % self.n
        return self.i


def build_program(nlayers=DEPTH):
    nc = bass.Bass("TRN2", target_bir_lowering=False)

    def din(name, shape):
        return nc.dram_tensor(name, list(shape), F32, kind="ExternalInput").ap()

    def dout(name, shape):
        return nc.dram_tensor(name, list(shape), F32, kind="ExternalOutput").ap()

    xp = din("xp", [S, D])
    xs_in = din("xs", [NB, D])
    caches = [din("c128", [DEPTH, NB, 128, 512]), din("c512", [DEPTH, NB, 512, 512]),
              din("c2048", [DEPTH, NB, 2048, 512])]
    w_in = din("w_in", [DEPTH, 128, 8, N_IN])
    wpa_d = din("wpa", [DEPTH, 128, 2, D])
    wpb_d = din("wpb", [DEPTH, 128, 6, D])
    wout_d = din("wout", [DEPTH, 128, 8, D])
    gpreT_d = din("gpreT", [128, DEPTH, 8])
    gpost_d = din("gpost", [DEPTH, D])
    bm_d = din("bm", [128, DEPTH, 16])
    gam_d = din("gam", [DEPTH, 768])
    beta_d = din("beta", [DEPTH, 768])
    wsT_d = din("wsT", [DEPTH, 128, 4, 128])
    bsT_d = din("bsT", [DEPTH, 128, 6, 128])
    ws00_d = din("ws00", [DEPTH * 4])
    bs0_d = din("bs0", [DEPTH * 4])
    ident_d = din("ident", [128, 128])
    mask_d = din("mask", [128, 256])
    maskbias_d = din("maskbias", [128, 256])
    cs_d = din("cs", [3, 128, 16, 32])
    css_d = din("css", [NB, 32])
    sel_d = din("sel", [NB, NB, 128])
    oh_d = din("oh", [128, NB, NB])

    y_p = dout("y_p", [S, D])
    y_s = dout("y_s", [NB, D])
    kvp = [dout("kv128p", [DEPTH, 128, 512]), dout("kv512p", [DEPTH, 512, 512]),
           dout("kv2048p", [DEPTH, 2048, 512])]
    kvs = [dout("kv128s", [DEPTH, NB, 512]), dout("kv512s", [DEPTH, NB, 512]),
           dout("kv2048s", [DEPTH, NB, 512])]
    gv_s = dout("gv_s", [DEPTH, NB, 768])
    xscr = nc.dram_tensor("xscr", [S, D], F32).ap()
    qscr = nc.dram_tensor("qscr", [DEPTH, NB, 768], F32).ap()

    f = FW(nc)
    PE, ACT, DVE, POOL, SP = f.pe, f.act, f.dve, f.pool, f.sp

    hT = f.sbuf("hT", [128, 8, S], BF16)
    hTs = f.sbuf("hTs", [128, 8, NB], BF16)
    Wc = [f.sbuf(f"Wc{i}", [128, 8, 768], BF16) for i in range(3)]
    wpa = f.sbuf("wpa", [128, 2, D], BF16)
    PB = f.sbuf("PB", [128, 32768], BF16)
    stg = [f.sbuf(f"stg{i}", [128, 1024], F32) for i in range(5)]
    hb = [f.sbuf(f"hb{i}", [128, 1024], BF16) for i in range(2)]
    junk = f.sbuf("junk", [128, 1024], BF16)
    gpost = f.sbuf("gpost", [128, D], F32)
    gam = f.sbuf("gamt", [128, 768], F32)
    beta = f.sbuf("betat", [128, 768], F32)
    bsT = f.sbuf("bsTt", [128, 6, 128], F32)
    wsT32 = f.sbuf("wsT32", [128, 4, 128], F32)
    wsT = f.sbuf("wsTb", [128, 4, 128], BF16)
    maskb = f.sbuf("maskb", [128, 256], BF16)
    mask32 = f.sbuf("mask32", [128, 128], F32)
    identb = f.sbuf("identb", [128, 128], BF16)
    ones64 = f.sbuf("ones64", [128, 64], BF16)
    css = f.sbuf("csst", [NB, 32], F32)
    gpreT = f.sbuf("gpreTt", [128, DEPTH, 8], F32)
    bm = f.sbuf("bmt", [128, DEPTH, 16], F32)
    ws00 = f.sbuf("ws00t", [NB, DEPTH * 4], F32)
    bs0 = f.sbuf("bs0t", [NB, DEPTH * 4], F32)
    sel = f.sbuf("selt", [NB, NB, 128], F32)
    oh = f.sbuf("oht", [128, NB, NB], F32)
    epsT = f.sbuf("epsT", [128, 1], F32)
    NACC = 320
    NSTAT = NACC + 128 + 96
    stat = f.sbuf("stat", [128, NSTAT], F32)
    xs_t = f.sbuf("xs_t", [NB, D], F32)
    zs = f.sbuf("zs", [NB, 2304], F32)
    nd = f.sbuf("nd", [NB, 800], F32)
    sga = f.sbuf("sga", [NB, 256], F32)
    sm = f.sbuf("sm", [NB, 512], F32)
    hsb = f.sbuf("hsb", [NB, D], BF16)
    gaTs = f.sbuf("gaTs", [128, 2, NB], BF16)
    ugTs = f.sbuf("ugTs", [128, 6, NB], BF16)
    mTs = f.sbuf("mTs", [128, 8, NB], BF16)

    gaT = PB[:, 0:4096].rearrange("p (a t) -> p a t", a=2)
    qkT = PB[:, 4096:16384].rearrange("p (g c t) -> p g c t", g=3, c=2)
    Vt = PB[:, 16384:22528].rearrange("p (g t e) -> p g t e", g=3, t=16)
    accO = PB[:, 22528:26624].bitcast(F32)
    rec = PB[:, 26624:28672].bitcast(F32).rearrange("p (a t) -> p a t", a=2)
    Eb = PB[:, 28672:29696].rearrange("p (a t) -> p a t", a=4)
    qkb = PB[:, 29696:30720].rearrange("p (a t) -> p a t", a=4)
    cs2 = [PB[:, 30720 + i * 1024:30720 + (i + 1) * 1024].bitcast(F32).rearrange("p (t e) -> p t e", t=16) for i in range(2)]
    hbF = [PB[:, i * 1024:(i + 1) * 1024] for i in range(4)]
    ugT = PB[:, 4096:16384].rearrange("p (a t) -> p a t", a=6)
    vn = PB[:, 16384:28672].rearrange("p (t e) -> p t e", t=16)
    mT = PB[:, 16384:32768].rearrange("p (a t) -> p a t", a=8)
    tmpb = PB[:, 30720:31744].rearrange("p (a t) -> p a t", a=2)

    psA = [f.psum(f"psA{i}", [128, 1024], F32) for i in range(3)]
    psT = [f.psum(f"psT{i}", [128, 1024], BF16) for i in range(2)]

    def bank(i):
        return psA[i // 2][:, (i % 2) * 512:(i % 2) * 512 + 512]

    def bkeys(i):
        return [("ph", i, 0), ("ph", i, 1)]

    rot_b = Rot(6)
    rot_d = Rot(3)
    rot_t = Rot(2)
    rot_stg = Rot(5)
    rot_hb = Rot(2)
    rot_wc = Rot(3)
    stat_i = [0]
    rot_s1 = Rot(128)
    rot_s12 = Rot(8)

    def acc_stat(n=1):
        i = stat_i[0]
        stat_i[0] += n
        assert stat_i[0] <= NACC
        return i

    def new_stat(n=1):
        if n == 1:
            return NACC + rot_s1.next()
        assert n <= 12
        return NACC + 128 + 12 * rot_s12.next()

    f.dma(SP, gpreT[:], gpreT_d, [], ["gpreT"])
    f.dma(SP, bm[:], bm_d, [], ["bm"])
    f.dma(SP, mask32[:], mask_d[:, 128:256], [], ["mask32"])
    f.dma(SP, css[:], css_d, [], ["css"])
    f.dma(SP, sel[:], sel_d, [], ["sel"])
    f.dma(SP, oh[:], oh_d, [], ["oh"])
    f.dma(SP, ws00[:], ws00_d.partition_broadcast(NB), [], ["ws00"])
    f.dma(SP, bs0[:], bs0_d.partition_broadcast(NB), [], ["bs0"])
    f.dma(SP, xs_t[:], xs_in, [], ["xs_t"])
    f.dma(POOL, identb[:], ident_d, [], ["identb"])
    f.dma(POOL, maskb[:], mask_d, [], ["maskb"])
    f.op(DVE, [], ["ones64"], lambda: nc.vector.memset(ones64[:], 1.0))
    f.op(DVE, [], ["epsT"], lambda: nc.vector.memset(epsT[:], EPS))
    f.op(DVE, [], ["stat"], lambda: nc.vector.memset(stat[:], 0.0))

    preloaded = {}

    def _issue_chunk(l, off, ncols, avoid=()):
        i = rot_wc.next()
        while i in avoid:
            i = rot_wc.next()
        f.dma(POOL, Wc[i][:, :, 0:ncols], w_in[l, :, :, off:off + ncols], [], [("Wc", i)])
        return i

    def load_chunk(l, off, ncols, avoid=()):
        if (l, off) in preloaded:
            return preloaded.pop((l, off))
        return _issue_chunk(l, off, ncols, avoid)

    def preload_chunk(l, off, ncols, avoid=()):
        if l < nlayers:
            preloaded[(l, off)] = _issue_chunk(l, off, ncols, avoid)

    def rstd_from_ss(ss_slot, scale, P, n=1):
        sd = new_stat(n)
        rs = new_stat(n)
        f.op(ACT, [("st", ss_slot), "epsT"], [("st", sd)],
             lambda: nc.scalar.activation(stat[0:P, sd:sd + n], stat[0:P, ss_slot:ss_slot + n], AF.Sqrt,
                                          bias=epsT[0:P, 0:1], scale=scale))
        f.op(DVE, [("st", sd)], [("st", rs)],
             lambda: nc.vector.reciprocal(stat[0:P, rs:rs + n], stat[0:P, sd:sd + n]))
        return rs

    rot_hbF = Rot(4)

    def prenorm_a(x_ap, xkey, P):
        ss = acc_stat()
        f.op(ACT, [xkey, "stat"], [("st", ss), "junk"],
             lambda: nc.scalar.activation(junk[0:P, :], x_ap, AF.Square, accum_out=stat[0:P, ss:ss + 1]))
        rs = rstd_from_ss(ss, 1.0 / D, P)
        if P == 128:
            hi = rot_hbF.next()
            hap, hkey = hbF[hi], ("hbF", hi)
        else:
            hap, hkey = hsb[:, :], "hsb"
        f.op(ACT, [xkey, ("st", rs)], [hkey],
             lambda: nc.scalar.activation(hap[0:P, :], x_ap, AF.Copy, scale=stat[0:P, rs:rs + 1]))
        return hap, hkey

    def prenorm_b(l, hh_, P, hT_dst, hT_key):
        hap, hkey = hh_
        ti = rot_t.next()

        def tr():
            for kt in range(8):
                ins = nc.tensor.transpose(psT[ti][:, kt * P:(kt + 1) * P], hap[0:P, kt * 128:(kt + 1) * 128],
                                          identb[0:P, 0:P])
            return ins
        f.op(PE, [hkey, "identb"], [("pt", ti)], tr)
        f.op(DVE, [("pt", ti), "gpreT"], [hT_key],
             lambda: nc.vector.tensor_tensor(hT_dst, psT[ti][:, 0:8 * P].rearrange("p (k t) -> p k t", k=8),
                                             gpreT[:, l, :].unsqueeze(2).to_broadcast([128, 8, P]), ALU.mult))

    def prenorm(l, x_ap, xkey, P, hT_dst, hT_key):
        hi = prenorm_a(x_ap, xkey, P)
        prenorm_b(l, hi, P, hT_dst, hT_key)

    def load_layer_small(l):
        f.dma(SP, gam[:], gam_d[l].partition_broadcast(128), [], ["gam"])
        f.dma(SP, beta[:], beta_d[l].partition_broadcast(128), [], ["beta"])
        f.dma(SP, bsT[:], bsT_d[l], [], ["bsT"])
        f.dma(SP, wsT32[:], wsT_d[l], [], ["wsT32"])

    hT_keys = [("hT", t) for t in range(16)]
    ZK = [("zs", c) for c in range(6)]
    MT_KEYS = [("mT", a) for a in range(8)]
    UG_KEYS = [("ugT", a, n) for a in range(6) for n in range(4)]

    for tt in range(16):
        si = rot_stg.next()
        f.dma(SP, stg[si][:], xp[tt * 128:(tt + 1) * 128, :], [], [("stg", si)])
        prenorm(0, stg[si][:], ("stg", si), 128, hT[:, :, tt * 128:(tt + 1) * 128], ("hT", tt))
    prenorm(0, xs_t[:], "xs_t", NB, hTs[:], "hTs")

    for l in range(nlayers):
        last = l == nlayers - 1
        load_layer_small(l)
        HBF_KEYS = [("hbF", i) for i in range(4)]
        f.sync_on(ACT, MT_KEYS + UG_KEYS + HBF_KEYS)
        f.sync_on(DVE, MT_KEYS + UG_KEYS + HBF_KEYS)
        f.sync_on(SP, MT_KEYS)

        for sp in range(2):
            for g in range(3):
                dl = DIL[g]
                nblk = (S // dl) // 128
                ci = load_chunk(l, OFF_A + (sp * 3 + g) * 384, 384)
                csi = (sp * 3 + g) % 2
                cs = cs2[csi]
                cskey = ("cs", csi)
                f.dma(SP, cs, cs_d[g], [], [cskey])

                st1 = {}

                def stage1(pt):
                    r, mb = divmod(pt, nblk)
                    start = r + dl * mb * 128
                    stop = start + dl * 127 + 1
                    bi = rot_b.next()
                    rkeys = list(hT_keys) if dl > 1 else [("hT", pt)]

                    def mmA():
                        for kt in range(8):
                            ins = nc.tensor.matmul(bank(bi)[:, 0:384], hT[:, kt, start:stop:dl], Wc[ci][:, kt, 0:384],
                                                   start=(kt == 0), stop=(kt == 7))
                        return ins
                    f.op(PE, rkeys + [("Wc", ci)], bkeys(bi), mmA)
                    si = rot_stg.next()
                    f.op(ACT, bkeys(bi), [("stg", si)],
                         lambda: nc.scalar.activation(stg[si][:, 0:384], bank(bi)[:, 0:384], AF.Copy))
                    f.op(ACT, bkeys(bi), [("V", g, pt)],
                         lambda: nc.scalar.activation(Vt[:, g, pt, :], bank(bi)[:, 256:384], AF.Copy))
                    qk = stg[si][:, 0:256].rearrange("p (h d) -> p h d", h=4)[:, :, 0:16]
                    A_ = stg[si][:, 512:576].rearrange("p (h d) -> p h d", h=4)
                    B_ = stg[si][:, 576:640].rearrange("p (h d) -> p h d", h=4)
                    cc = cs[:, pt, 0:16].unsqueeze(1).to_broadcast([128, 4, 16])
                    sn = cs[:, pt, 16:32].unsqueeze(1).to_broadcast([128, 4, 16])
                    k1, k2 = ("stgA", si), ("stgB", si)
                    f.op(DVE, [("stg", si), cskey], [k1], lambda: nc.vector.tensor_tensor(A_, qk, cc, ALU.mult))
                    f.op(DVE, [("stg", si), cskey], [k2], lambda: nc.vector.tensor_tensor(B_, qk, sn, ALU.mult))
                    f.op(DVE, [k1, k2], [("stg", si)],
                         lambda: nc.vector.tensor_tensor(qk[:, :, 0:8], A_[:, :, 0:8], B_[:, :, 8:16], ALU.subtract))
                    f.op(DVE, [k1, k2], [("stg", si)],
                         lambda: nc.vector.tensor_tensor(qk[:, :, 8:16], A_[:, :, 8:16], B_[:, :, 0:8], ALU.add))
                    keep = (mb == nblk - 1) if g < 2 else True
                    if keep:
                        if g == 0:
                            rows = kvp[0][l, :, :]
                        elif g == 1:
                            rows = kvp[1][l, r:512:4, :]
                        else:
                            rows = kvp[2][l, r:2048:16, :]
                        dst = rows.rearrange("t (kv h e) -> t kv h e", kv=2, h=4)[:, :, 2 * sp:2 * sp + 2, :]
                        src = stg[si][:, 128:384].rearrange("p (kv h e) -> p kv h e", kv=2, h=2)
                        f.dma(SP, dst, src, [("stg", si)], [], is_output=True)
                    st1[pt] = si

                def stage1b(pt):
                    si = st1.pop(pt)
                    qi = pt % 4
                    f.op(ACT, [("stg", si)], [("qkb", qi)], lambda: nc.scalar.activation(qkb[:, qi, :], stg[si][:, 0:256], AF.Copy))

                def stage2(pt):
                    qi = pt % 4
                    ti = rot_t.next()

                    def trA():
                        nc.tensor.transpose(psT[ti][:, 0:128], qkb[:, qi, 0:128], identb[:])
                        return nc.tensor.transpose(psT[ti][:, 128:256], qkb[:, qi, 128:256], identb[:])
                    f.op(PE, [("qkb", qi), "identb"], [("pt", ti)], trA)
                    f.op(ACT, [("pt", ti)], [("qkT", g, pt)],
                         lambda: nc.scalar.activation(qkT[:, g, :, pt * 128:(pt + 1) * 128],
                                                      psT[ti][:, 0:256].rearrange("p (c t) -> p c t", c=2), AF.Copy))
                LA = 3
                for i in range(16 + LA):
                    if i < 16:
                        stage1(i)
                    if 0 <= i - 1 < 16:
                        stage1b(i - 1)
                    if i - LA >= 0:
                        stage2(i - LA)
                bi = rot_b.next()

                def mmAs():
                    for kt in range(8):
                        ins = nc.tensor.matmul(bank(bi)[0:NB, 0:384], hTs[:, kt, :], Wc[ci][:, kt, 0:384],
                                               start=(kt == 0), stop=(kt == 7))
                    return ins
                f.op(PE, ["hTs", ("Wc", ci)], bkeys(bi), mmAs)
                c6 = sp * 3 + g
                f.op(ACT, bkeys(bi), [("zs", c6)],
                     lambda: nc.scalar.activation(zs[:, c6 * 384:(c6 + 1) * 384], bank(bi)[0:NB, 0:384], AF.Copy))

            if sp == 0:
                ci = load_chunk(l, OFF_GA, 256)
                for mt in range(2):
                    for nt in range(4):
                        bi = rot_b.next()

                        def mmB():
                            for kt in range(8):
                                ins = nc.tensor.matmul(bank(bi), Wc[ci][:, kt, mt * 128:(mt + 1) * 128],
                                                       hT[:, kt, nt * 512:(nt + 1) * 512], start=(kt == 0), stop=(kt == 7))
                            return ins
                        f.op(PE, hT_keys[nt * 4:nt * 4 + 4] + [("Wc", ci)], bkeys(bi), mmB)
                        f.op(ACT, bkeys(bi), [("gaT", mt, nt)],
                             lambda: nc.scalar.activation(gaT[:, mt, nt * 512:(nt + 1) * 512], bank(bi), AF.Silu))
                bi = rot_b.next()

                def mmBs():
                    for kt in range(8):
                        ins = nc.tensor.matmul(bank(bi)[0:NB, 0:256], hTs[:, kt, :], Wc[ci][:, kt, 0:256],
                                               start=(kt == 0), stop=(kt == 7))
                    return ins
                f.op(PE, ["hTs", ("Wc", ci)], bkeys(bi), mmBs)
                f.op(ACT, bkeys(bi), ["sga"], lambda: nc.scalar.activation(sga[:], bank(bi)[0:NB, 0:256], AF.Silu))

            if sp == 0:
                preload_chunk(l, OFF_A + 3 * 384, 384)
            else:
                preload_chunk(l, OFF_VB, 768)
            jobs = [(hh, g, pt) for hh in range(2) for g in range(3) for pt in range(16)]
            jstate = {}

            def c_stage1(j):
                hh, g, pt = jobs[j]
                nblk = (S // DIL[g]) // 128
                r, mb = divmod(pt, nblk)
                kbs = [pt - 1, pt] if mb > 0 else [pt]
                nk = len(kbs)
                sj = 2 + j % 4
                Sap = bank(sj)[:, 0:nk * 128]
                skey = bkeys(sj)

                def mmS():
                    for i, kb in enumerate(kbs):
                        ins = nc.tensor.matmul(Sap[:, i * 128:(i + 1) * 128],
                                               qkT[64 * hh:64 * hh + 64, g, 1, kb * 128:(kb + 1) * 128],
                                               qkT[64 * hh:64 * hh + 64, g, 0, pt * 128:(pt + 1) * 128],
                                               start=True, stop=True)
                    return ins
                f.op(PE, [("qkT", g, kb) for kb in kbs], skey, mmS)
                ei = j % 4
                Eap = Eb[:, ei, 0:nk * 128]
                f.op(ACT, skey, [("Eb", ei)], lambda: nc.scalar.activation(Eap, Sap, AF.Exp, scale=0.125))
                mk = maskb[:, 0:256] if nk == 2 else maskb[:, 128:256]
                if j % 3 == 2:
                    f.op(DVE, [("Eb", ei), "maskb"], [("Eb", ei)], lambda: nc.vector.tensor_tensor(Eap, Eap, mk, ALU.mult))
                else:
                    f.op(POOL, [("Eb", ei), "maskb"], [("Eb", ei)], lambda: nc.gpsimd.tensor_tensor(Eap, Eap, mk, ALU.mult))
                jstate[j] = (kbs, ei, Eap)

            def c_stage2(j):
                hh, g, pt = jobs[j]
                kbs, ei, Eap = jstate.pop(j)
                nb_ = 64 * hh
                db_ = 64 * (1 - hh)
                b4, qq = divmod(pt, 4)
                oi = (j // 4) % 2
                Ocols = slice(qq * 128, (qq + 1) * 128)
                nkk = len(kbs)

                def mmO():
                    for i, kb in enumerate(kbs):
                        nc.tensor.matmul(bank(oi)[nb_:nb_ + 64, Ocols], Vt[:, g, kb, 64 * hh:64 * hh + 64],
                                         Eap[:, i * 128:(i + 1) * 128], start=(i == 0), stop=(i == nkk - 1))
                    for i, kb in enumerate(kbs):
                        ins = nc.tensor.matmul(bank(oi)[db_:db_ + 64, Ocols], ones64[:],
                                               Eap[:, i * 128:(i + 1) * 128], start=(i == 0), stop=(i == nkk - 1))
                    return ins
                rd = [("Eb", ei), "ones64"] + [("V", g, kb) for kb in kbs]
                f._pre(PE, rd, bkeys(oi) if qq == 0 else [])
                ins = mmO()
                ev = PE.bump()
                ins.then_inc(ev[0], 1)
                f.ninstr["pe"] += 1
                f._post(ev, rd, bkeys(oi) if qq == 3 else [])
                if qq < 3:
                    return
                AK = [("accO", q) for q in range(4)]
                if g == 0:
                    f.op(DVE, bkeys(oi), [("accO", b4)],
                         lambda: nc.vector.tensor_copy(accO[:, b4 * 512:(b4 + 1) * 512], bank(oi)))
                elif g == 1:
                    dstv = accO[:, b4:2048:4]
                    f.op(DVE, bkeys(oi) + AK, AK, lambda: nc.vector.tensor_tensor(dstv, bank(oi), dstv, ALU.add))
                else:
                    dstv = accO.rearrange("p (m r) -> p r m", r=16)[:, 4 * b4:4 * b4 + 4, :]
                    srcv = bank(oi).rearrange("p (r m) -> p r m", r=4)
                    f.op(DVE, bkeys(oi) + AK, AK, lambda: nc.vector.tensor_tensor(dstv, srcv, dstv, ALU.add))
                if g == 2 and pt == 15:
                    f.op(ACT, AK, AK, lambda: nc.scalar.activation(accO[db_:db_ + 64, :], accO[db_:db_ + 64, :], AF.Ln))
                    for nt in range(4):
                        ri = nt % 2
                        cols = slice(nt * 512, (nt + 1) * 512)
                        f.op(ACT, AK, [("rec", ri)],
                             lambda: nc.scalar.activation(rec[nb_:nb_ + 64, ri, :], accO[db_:db_ + 64, cols], AF.Exp, scale=-1.0))
                        f.op(DVE, AK + [("rec", ri)], [("rec", ri)],
                             lambda: nc.vector.tensor_tensor(rec[nb_:nb_ + 64, ri, :], accO[nb_:nb_ + 64, cols],
                                                             rec[nb_:nb_ + 64, ri, :], ALU.mult))
                        f.op(DVE, [("rec", ri), ("gaT", sp, nt)], [("gaT", sp, nt)],
                             lambda: nc.vector.tensor_tensor(gaT[nb_:nb_ + 64, sp, cols], rec[nb_:nb_ + 64, ri, :],
                                                             gaT[nb_:nb_ + 64, sp, cols], ALU.mult))
            redA_ap = psT[0][:].bitcast(F32)
            redB_ap = psT[1][:].bitcast(F32)
            RED_KEYS = [("pt", 0), ("pt", 1)]
            zq = zs[:].rearrange("p (c t e) -> p c t e", c=6, t=3)

            def SA_pre():
                for t_ in range(2):
                    v16 = zs[:].rearrange("p (c t h d) -> p c t h d", c=6, t=3, h=2)[:, :, t_, :, 0:16]
                    A_ = sm[:, 0:192].rearrange("p (c h d) -> p c h d", c=6, h=2)
                    B_ = sm[:, 192:384].rearrange("p (c h d) -> p c h d", c=6, h=2)
                    cc = css[:, 0:16].unsqueeze(1).unsqueeze(1).to_broadcast([NB, 6, 2, 16])
                    sn = css[:, 16:32].unsqueeze(1).unsqueeze(1).to_broadcast([NB, 6, 2, 16])
                    f.op(DVE, ZK + ["css"], ["smA"], lambda: nc.vector.tensor_tensor(A_, v16, cc, ALU.mult))
                    f.op(DVE, ZK + ["css"], ["smB"], lambda: nc.vector.tensor_tensor(B_, v16, sn, ALU.mult))
                    f.op(DVE, ["smA", "smB"], ZK,
                         lambda: nc.vector.tensor_tensor(v16[:, :, :, 0:8], A_[:, :, :, 0:8], B_[:, :, :, 8:16], ALU.subtract))
                    f.op(DVE, ["smA", "smB"], ZK,
                         lambda: nc.vector.tensor_tensor(v16[:, :, :, 8:16], A_[:, :, :, 8:16], B_[:, :, :, 0:8], ALU.add))
                for g in range(3):
                    for kv in range(2):
                        src = zs[:].rearrange("p (s g t e) -> p g t s e", s=2, g=3, t=3)[:, g, 1 + kv, :, :]
                        dst = kvs[g][l][:, kv * 256:(kv + 1) * 256].rearrange("b (s e) -> b s e", s=2)
                        f.dma(SP, dst, src, ZK, [], is_output=True)

                f.dma(SP, qscr[l].rearrange("b (c e) -> b c e", c=6), zq[:, :, 0, :], ZK, [("qscr", l)], semkey=("qscr", l))
                f._pre(PE, [], RED_KEYS)

            def SA_S1(b):
                for g in range(3):
                    dl = DIL[g]
                    L = 128 * dl
                    srcKV = caches[g][l, b, 0:L:dl, :].rearrange("j (kv s e) -> j kv s e", kv=2, s=2)
                    dK = stg[0][:, 0:768].rearrange("p (s g e) -> p g s e", s=2, g=3)[:, g, :, :]
                    dV = stg[1][:, 0:768].rearrange("p (s g e) -> p g s e", s=2, g=3)[:, g, :, :]
                    f.dma(SP, dK, srcKV[:, 0, :, :], [], [("stg", 0)])
                    f.dma(SP, dV, srcKV[:, 1, :, :], [], [("stg", 1)])
                f.dma(SP, stg[3][:, 0:768], qscr[l, b].partition_broadcast(128), [("qscr", l)], [("stg", 3)])
                f.op(POOL, [("stg", 0), ("stg", 3)], [("stg", 2)],
                     lambda: nc.gpsimd.tensor_tensor(stg[2][:, 0:768], stg[0][:, 0:768], stg[3][:, 0:768], ALU.mult))
                sc = new_stat(12)
                f.op(DVE, [("stg", 2)], [("st", sc)],
                     lambda: nc.vector.tensor_reduce(stat[:, sc:sc + 12], stg[2][:, 0:768].rearrange("p (a d) -> p a d", d=64),
                                                     AX.X, ALU.add))
                f.op(ACT, [("st", sc)], [("stgE", 2)],
                     lambda: nc.scalar.activation(stg[2][:, 768:780], stat[:, sc:sc + 12], AF.Exp, scale=0.125))
                f.op(POOL, [("stg", 1), ("stgE", 2)], [("stg", 2)],
                     lambda: nc.gpsimd.tensor_tensor(
                         stg[2][:, 0:768].rearrange("p (a d) -> p a d", d=64), stg[1][:, 0:768].rearrange("p (a d) -> p a d", d=64),
                         stg[2][:, 768:780].unsqueeze(2).to_broadcast([128, 12, 64]), ALU.mult))

            def SA_S2(b):
                def mmr():
                    nc.tensor.matmul(redA_ap[0:NB, 0:512], oh[:, b, :], stg[2][:, 0:512], start=(b == 0), stop=(b == NB - 1))
                    return nc.tensor.matmul(redB_ap[0:NB, 0:268], oh[:, b, :], stg[2][:, 512:780], start=(b == 0), stop=(b == NB - 1))
                rd = [("stg", 2), ("stgE", 2), "oh"]
                f._pre(PE, rd, [])
                ins = mmr()
                ev = PE.bump()
                ins.then_inc(ev[0], 1)
                f.ninstr["pe"] += 1
                f._post(ev, rd, RED_KEYS if b == NB - 1 else [])

            def SA_post():
                f.op(DVE, [("pt", 0)], ["nd"], lambda: nc.vector.tensor_copy(nd[:, 0:512], redA_ap[0:NB, 0:512]))
                f.op(DVE, [("pt", 1)], ["nd"], lambda: nc.vector.tensor_copy(nd[:, 512:780], redB_ap[0:NB, 0:268]))
                sp_i = 4
                selfp = stg[sp_i][0:NB, 0:768]
                f.op(DVE, ZK, [("stg", sp_i)],
                     lambda: nc.vector.tensor_tensor(selfp.rearrange("p (c e) -> p c e", c=6), zq[:, :, 0, :], zq[:, :, 1, :], ALU.mult))
                ssc = new_stat(12)
                f.op(DVE, [("stg", sp_i)], [("st", ssc)],
                     lambda: nc.vector.tensor_reduce(stat[0:NB, ssc:ssc + 12], selfp.rearrange("p (a d) -> p a d", d=64), AX.X, ALU.add))
                ses = new_stat(12)
                f.op(ACT, [("st", ssc)], [("st", ses)],
                     lambda: nc.scalar.activation(stat[0:NB, ses:ses + 12], stat[0:NB, ssc:ssc + 12], AF.Exp, scale=0.125))
                f.op(DVE, ZK + [("st", ses)], [("stg", sp_i)],
                     lambda: nc.vector.tensor_tensor(
                         selfp.rearrange("p (c h d) -> p c h d", c=6, h=2),
                         zs[:].rearrange("p (c t h d) -> p c t h d", c=6, t=3, h=2)[:, :, 2, :, :],
                         stat[0:NB, ses:ses + 12].rearrange("p (c h) -> p c h", c=6).unsqueeze(3).to_broadcast([NB, 6, 2, 64]), ALU.mult))
                f.op(DVE, [("stg", sp_i), "nd"], ["nd"], lambda: nc.vector.tensor_tensor(nd[:, 0:768], nd[:, 0:768], selfp, ALU.add))
                f.op(DVE, [("st", ses), "nd"], ["nd"],
                     lambda: nc.vector.tensor_tensor(nd[:, 768:780], nd[:, 768:780], stat[0:NB, ses:ses + 12], ALU.add))
                ndv = nd[:, 0:768].rearrange("p (s g e) -> p s g e", s=2, g=3)
                dnv = nd[:, 768:780].rearrange("p (s g h) -> p s g h", s=2, g=3)
                nt_ = sm[:, 0:256].rearrange("p (s e) -> p s e", s=2)
                dt_ = sm[:, 256:260].rearrange("p (s h) -> p s h", s=2)
                f.op(DVE, ["nd", "smA", "smB"], ["smN"], lambda: nc.vector.tensor_tensor(nt_, ndv[:, :, 0, :], ndv[:, :, 1, :], ALU.add))
                f.op(DVE, ["nd", "smN"], ["smN"], lambda: nc.vector.tensor_tensor(nt_, nt_, ndv[:, :, 2, :], ALU.add))
                f.op(DVE, ["nd", "smB"], ["smD"], lambda: nc.vector.tensor_tensor(dt_, dnv[:, :, 0, :], dnv[:, :, 1, :], ALU.add))
                f.op(DVE, ["nd", "smD"], ["smD"], lambda: nc.vector.tensor_tensor(dt_, dt_, dnv[:, :, 2, :], ALU.add))
                f.op(DVE, ["smD"], ["smD"], lambda: nc.vector.reciprocal(sm[:, 256:260], sm[:, 256:260]))
                f.op(DVE, ["smN", "smD"], ["smN"],
                     lambda: nc.vector.tensor_tensor(sm[:, 0:256].rearrange("p (h d) -> p h d", h=4), sm[:, 0:256].rearrange("p (h d) -> p h d", h=4),
                                                     sm[:, 256:260].unsqueeze(2).to_broadcast([NB, 4, 64]), ALU.mult))
                f.op(DVE, ["smN", "sga"], ["hsb"], lambda: nc.vector.tensor_tensor(hsb[:, 0:256], sm[:, 0:256], sga[:], ALU.mult))
                ti = rot_t.next()

                def trga():
                    nc.tensor.transpose(psT[ti][:, 0:NB], hsb[:, 0:128], identb[0:NB, 0:NB])
                    return nc.tensor.transpose(psT[ti][:, NB:2 * NB], hsb[:, 128:256], identb[0:NB, 0:NB])
                f.op(PE, ["hsb", "identb"], [("pt", ti)], trga)
                f.op(DVE, [("pt", ti)], ["gaTs"],
                     lambda: nc.vector.tensor_copy(gaTs[:], psT[ti][:, 0:2 * NB].rearrange("p (k t) -> p k t", k=2)))

            sa_sched = {}
            if sp == 1:
                sa_sched = {2: [SA_pre], 8: [lambda: SA_S1(0)], 24: [lambda: SA_S2(0), lambda: SA_S1(1)],
                            40: [lambda: SA_S2(1), lambda: SA_S1(2)], 56: [lambda: SA_S2(2), lambda: SA_S1(3)],
                            72: [lambda: SA_S2(3)], 80: [SA_post]}
            LA = 3
            for i in range(len(jobs) + LA):
                if i < len(jobs):
                    c_stage1(i)
                if i - LA >= 0:
                    c_stage2(i - LA)
                for fn_ in sa_sched.get(i, []):
                    fn_()


        pbD_keys = [("qkT", g, pt) for g in range(3) for pt in range(16)] + [("V", g, pt) for g in range(3) for pt in range(16)] + \
                   [("accO", q) for q in range(4)] + [("rec", i) for i in range(2)] + [("Eb", i) for i in range(4)] + \
                   [("qkb", i) for i in range(4)] + [("cs", i) for i in range(2)]
        f.sync_on(ACT, pbD_keys + ZK)
        f.sync_on(DVE, pbD_keys + ZK)
        f.op(DVE, ["wsT32", "mask32"], ["wsT"],
             lambda: nc.vector.tensor_tensor(wsT[:], wsT32[:], mask32[:].unsqueeze(1).to_broadcast([128, 4, 128]), ALU.mult))
        ci = load_chunk(l, OFF_VB, 768)
        preload_chunk(l, OFF_U, 768)

        def ln_chain(P, s1, s2, n=1):
            mean, msq, var = new_stat(n), new_stat(n), new_stat(n)
            f.op(DVE, [("st", s1)], [("st", mean)],
                 lambda: nc.vector.tensor_scalar(stat[0:P, mean:mean + n], stat[0:P, s1:s1 + n], 1.0 / 768, None, ALU.mult))
            f.op(DVE, [("st", mean)], [("st", msq)],
                 lambda: nc.vector.tensor_tensor(stat[0:P, msq:msq + n], stat[0:P, mean:mean + n], stat[0:P, mean:mean + n], ALU.mult))
            f.op(DVE, [("st", s2), ("st", msq)], [("st", var)],
                 lambda: nc.vector.scalar_tensor_tensor(stat[0:P, var:var + n], stat[0:P, s2:s2 + n], 1.0 / 768, stat[0:P, msq:msq + n],
                                                        ALU.mult, ALU.subtract))
            rs = rstd_from_ss(var, 1.0, P, n)
            nmr = new_stat(n)
            f.op(DVE, [("st", mean), ("st", rs)], [("st", nmr)],
                 lambda: nc.vector.scalar_tensor_tensor(stat[0:P, nmr:nmr + n], stat[0:P, mean:mean + n], -1.0, stat[0:P, rs:rs + n],
                                                        ALU.mult, ALU.mult))
            return rs, nmr

        def vn_tile(P, src_psum_ap, src_keys, g32, g32keys, out_ap, out_keys):
            s1, s2 = acc_stat(), acc_stat()
            f.op(ACT, src_keys + ["stat"], g32keys + [("st", s1)],
                 lambda: nc.scalar.activation(g32, src_psum_ap, AF.Gelu_apprx_tanh, accum_out=stat[0:P, s1:s1 + 1]))
            f.op(ACT, g32keys + ["stat"], [("st", s2), "junk"],
                 lambda: nc.scalar.activation(junk[0:P, 0:768], g32, AF.Square, accum_out=stat[0:P, s2:s2 + 1]))
            rs, nmr = ln_chain(P, s1, s2)
            f.op(DVE, g32keys + [("st", rs), ("st", nmr)], g32keys,
                 lambda: nc.vector.tensor_scalar(g32, g32, stat[0:P, rs:rs + 1], stat[0:P, nmr:nmr + 1], ALU.mult, ALU.add))
            f.op(DVE, g32keys + ["gam"], g32keys, lambda: nc.vector.tensor_tensor(g32, g32, gam[0:P, :], ALU.mult))
            f.op(DVE, g32keys + ["beta"], out_keys, lambda: nc.vector.tensor_tensor(out_ap, g32, beta[0:P, :], ALU.add))

        def mm768(P, lhs_fn, lhs_keys, ci):
            di = rot_d.next()

            def mm():
                for kt in range(8):
                    nc.tensor.matmul(psA[di][0:P, 0:512], lhs_fn(kt), Wc[ci][:, kt, 0:512], start=(kt == 0), stop=(kt == 7))
                    ins = nc.tensor.matmul(psA[di][0:P, 512:768], lhs_fn(kt), Wc[ci][:, kt, 512:768], start=(kt == 0), stop=(kt == 7))
                return ins
            dk = bkeys(2 * di) + bkeys(2 * di + 1)
            f.op(PE, lhs_keys + [("Wc", ci)], dk, mm)
            return di, dk

        Z0, Z1, Z2 = [("zs", 0), ("zs", 1)], [("zs", 2), ("zs", 3)], [("zs", 4), ("zs", 5)]
        di, dk = mm768(NB, lambda kt: hTs[:, kt, :], ["hTs"], ci)
        vn_tile(NB, psA[di][0:NB, 0:768], dk, zs[:, 0:768], Z0, zs[:, 0:768], Z0)
        f.dma(SP, gv_s[l], zs[:, 0:768], Z0, [], is_output=True)
        for g4 in range(4):
            f.op(DVE, Z0 + ["ws00", "bs0"], Z0,
                 lambda: nc.vector.tensor_scalar(zs[:, g4 * 192:(g4 + 1) * 192], zs[:, g4 * 192:(g4 + 1) * 192],
                                                 ws00[:, l * 4 + g4:l * 4 + g4 + 1], bs0[:, l * 4 + g4:l * 4 + g4 + 1],
                                                 ALU.mult, ALU.add))
        stgX = PB[:, 28672:30720].bitcast(F32)
        stgD = [(stg[i][:, 0:768], ("stg", i)) for i in range(5)] + [(stgX[:, 0:768], "stgX")]
        rotD = Rot(6)
        for g4 in range(4):
            s1b, s2b = acc_stat(4), acc_stat(4)
            bufs = []
            for c in range(4):
                tt = g4 * 4 + c
                di, dk = mm768(128, lambda kt: hT[:, kt, tt * 128:(tt + 1) * 128], [("hT", tt)], ci)
                g32, gkey = stgD[rotD.next()]
                f.op(ACT, dk + ["stat"], [gkey, ("st", s1b)],
                     lambda: nc.scalar.activation(g32, psA[di][:, 0:768], AF.Gelu_apprx_tanh, accum_out=stat[:, s1b + c:s1b + c + 1]))
                f.op(ACT, [gkey, "stat"], [("st", s2b), "junk"],
                     lambda: nc.scalar.activation(junk[:, 0:768], g32, AF.Square, accum_out=stat[:, s2b + c:s2b + c + 1]))
                bufs.append((g32, gkey, tt))
            rs4, nmr4 = ln_chain(128, s1b, s2b, 4)
            for c, (g32, gkey, tt) in enumerate(bufs):
                f.op(DVE, [gkey, ("st", rs4), ("st", nmr4)], [gkey],
                     lambda: nc.vector.tensor_scalar(g32, g32, stat[:, rs4 + c:rs4 + c + 1], stat[:, nmr4 + c:nmr4 + c + 1], ALU.mult, ALU.add))
                f.op(DVE, [gkey, "gam"], [gkey], lambda: nc.vector.tensor_tensor(g32, g32, gam[:, :], ALU.mult))
                f.op(POOL, [gkey, "beta"], [("vn", tt)], lambda: nc.gpsimd.tensor_tensor(vn[:, tt, :], g32, beta[:, :], ALU.add))
        ci = load_chunk(l, OFF_U, 768)
        for mt in range(6):
            for nt in range(4):
                bi = rot_b.next()

                def mmD2():
                    for kt in range(8):
                        ins = nc.tensor.matmul(bank(bi), Wc[ci][:, kt, mt * 128:(mt + 1) * 128], hT[:, kt, nt * 512:(nt + 1) * 512],
                                               start=(kt == 0), stop=(kt == 7))
                    return ins
                f.op(PE, hT_keys[nt * 4:nt * 4 + 4] + [("Wc", ci)], bkeys(bi), mmD2)
                f.op(ACT, bkeys(bi), [("ugT", mt, nt)],
                     lambda: nc.scalar.activation(ugT[:, mt, nt * 512:(nt + 1) * 512], bank(bi), AF.Gelu_apprx_tanh))
        di, dk = mm768(NB, lambda kt: hTs[:, kt, :], ["hTs"], ci)
        f.op(ACT, dk, Z1, lambda: nc.scalar.activation(zs[:, 768:1536], psA[di][0:NB, 0:768], AF.Gelu_apprx_tanh))
        f.op(DVE, Z0 + Z1, Z0, lambda: nc.vector.tensor_tensor(zs[:, 0:768], zs[:, 0:768], zs[:, 768:1536], ALU.mult))
        preload_chunk(l, OFF_GB, 768)
        for mt in range(6):
            for tg in range(4):
                bi = rot_b.next()

                def mmD3():
                    for c in range(4):
                        tt = tg * 4 + c
                        oc = slice(c * 128, (c + 1) * 128)
                        if mt in (1, 4):
                            ga_ = 0 if mt == 1 else 2
                            nc.tensor.matmul(bank(bi)[0:64, oc], vn[:, tt, mt * 128:mt * 128 + 64], wsT[:, ga_, :], start=True, stop=True)
                            ins = nc.tensor.matmul(bank(bi)[64:128, oc], vn[:, tt, mt * 128 + 64:mt * 128 + 128], wsT[:, ga_ + 1, :],
                                                   start=True, stop=True)
                        else:
                            g_ = (mt * 128) // 192
                            ins = nc.tensor.matmul(bank(bi)[:, oc], vn[:, tt, mt * 128:(mt + 1) * 128], wsT[:, g_, :], start=True, stop=True)
                    return ins
                f.op(PE, [("vn", tg * 4 + c) for c in range(4)] + ["wsT"], bkeys(bi), mmD3)
                si = rot_stg.next()
                f.op(DVE, bkeys(bi) + ["bsT"], [("stg", si)],
                     lambda: nc.vector.tensor_tensor(
                         stg[si][:, 0:512].rearrange("p (c i) -> p c i", c=4), bank(bi).rearrange("p (c i) -> p c i", c=4),
                         bsT[:, mt, :].unsqueeze(1).to_broadcast([128, 4, 128]), ALU.add))
                f.op(POOL, [("stg", si), ("ugT", mt, tg)], [("ugT", mt, tg)],
                     lambda: nc.gpsimd.tensor_tensor(ugT[:, mt, tg * 512:(tg + 1) * 512], stg[si][:, 0:512],
                                                     ugT[:, mt, tg * 512:(tg + 1) * 512], ALU.mult))
        ci = load_chunk(l, OFF_GB, 768)
        k_tb = 0
        for mt in range(6):
            for nt in range(4):
                bi = rot_b.next()

                def mmD4():
                    for kt in range(8):
                        ins = nc.tensor.matmul(bank(bi), Wc[ci][:, kt, mt * 128:(mt + 1) * 128], hT[:, kt, nt * 512:(nt + 1) * 512],
                                               start=(kt == 0), stop=(kt == 7))
                    return ins
                f.op(PE, hT_keys[nt * 4:nt * 4 + 4] + [("Wc", ci)], bkeys(bi), mmD4)
                tb = k_tb % 2
                k_tb += 1
                f.op(ACT, bkeys(bi), [("tmpb", tb)], lambda: nc.scalar.activation(tmpb[:, tb, :], bank(bi), AF.Silu))
                f.op(DVE, [("tmpb", tb), ("ugT", mt, nt)], [("ugT", mt, nt)],
                     lambda: nc.vector.tensor_tensor(ugT[:, mt, nt * 512:(nt + 1) * 512], tmpb[:, tb, :],
                                                     ugT[:, mt, nt * 512:(nt + 1) * 512], ALU.mult))
        di, dk = mm768(NB, lambda kt: hTs[:, kt, :], ["hTs"], ci)
        f.op(ACT, dk, Z2, lambda: nc.scalar.activation(zs[:, 1536:2304], psA[di][0:NB, 0:768], AF.Silu))
        f.op(DVE, Z0 + Z2, ["hsb"], lambda: nc.vector.tensor_tensor(hsb[:, 0:768], zs[:, 0:768], zs[:, 1536:2304], ALU.mult))
        ti = rot_t.next()

        def trug():
            for k in range(6):
                ins = nc.tensor.transpose(psT[ti][:, k * NB:(k + 1) * NB], hsb[:, k * 128:(k + 1) * 128], identb[0:NB, 0:NB])
            return ins
        f.op(PE, ["hsb", "identb"], [("pt", ti)], trug)
        f.op(DVE, [("pt", ti)], ["ugTs"],
             lambda: nc.vector.tensor_copy(ugTs[:], psT[ti][:, 0:6 * NB].rearrange("p (k t) -> p k t", k=6)))

        pbE_keys = [("vn", t) for t in range(16)] + [("tmpb", i) for i in range(2)] + ["stgX"]
        f.sync_on(DVE, pbE_keys)
        f.sync_on(ACT, pbE_keys)
        f.dma(SP, gpost[:], gpost_d[l].partition_broadcast(128), [], ["gpost"])
        f.dma(POOL, wpa[:], wpa_d[l], [], ["wpa"])
        wpbi = rot_wc.next()
        wpb = Wc[wpbi][:].rearrange("p k c -> p (k c)")[:, 0:6144].rearrange("p (k c) -> p k c", k=6)
        f.dma(POOL, wpb, wpb_d[l], [], [("Wc", wpbi)])
        k_sg = 0
        e_bufs = []
        for qd in range(4):
            ci = load_chunk(l, OFF_E + qd * 512, 512, avoid=(wpbi,))
            e_bufs.append(ci)
            for mfl in range(2):
                mf = qd * 2 + mfl
                for nt in range(5):
                    smp = nt == 4
                    ncol = NB if smp else 512
                    rh = hTs[:] if smp else hT[:, :, nt * 512:(nt + 1) * 512]
                    rga = gaTs[:] if smp else gaT[:, :, nt * 512:(nt + 1) * 512]
                    rug = ugTs[:] if smp else ugT[:, :, nt * 512:(nt + 1) * 512]
                    hk = ["hTs"] if smp else hT_keys[nt * 4:nt * 4 + 4]
                    gk = ["gaTs"] if smp else [("gaT", 0, nt), ("gaT", 1, nt)]
                    uk = ["ugTs"] if smp else [("ugT", a, nt) for a in range(6)]
                    bla, blb, bba, bbb = rot_b.next(), rot_b.next(), rot_b.next(), rot_b.next()

                    def mmL(bo, coff):
                        for kt in range(8):
                            ins = nc.tensor.matmul(bank(bo)[:, 0:ncol], Wc[ci][:, kt, coff + mfl * 128:coff + mfl * 128 + 128], rh[:, kt, :],
                                                   start=(kt == 0), stop=(kt == 7))
                        return ins
                    f.op(PE, hk + [("Wc", ci)], bkeys(bla), lambda: mmL(bla, 0))
                    f.op(PE, hk + [("Wc", ci)], bkeys(blb), lambda: mmL(blb, 256))

                    def mmBa():
                        for kt in range(2):
                            ins = nc.tensor.matmul(bank(bba)[:, 0:ncol], wpa[:, kt, mf * 128:(mf + 1) * 128], rga[:, kt, :],
                                                   start=(kt == 0), stop=(kt == 1))
                        return ins
                    f.op(PE, gk + ["wpa"], bkeys(bba), mmBa)

                    def mmBb():
                        for kt in range(6):
                            ins = nc.tensor.matmul(bank(bbb)[:, 0:ncol], wpb[:, kt, mf * 128:(mf + 1) * 128], rug[:, kt, :],
                                                   start=(kt == 0), stop=(kt == 5))
                        return ins
                    f.op(PE, uk + [("Wc", wpbi)], bkeys(bbb), mmBb)
                    sg = k_sg % 2
                    k_sg += 1
                    sa_ = hb[sg][:, 0:ncol]
                    sb_ = hb[sg][:, 512:512 + ncol]
                    f.op(ACT, bkeys(bla) + ["bm"], [("hbA", sg), ("hb", sg)],
                         lambda: nc.scalar.activation(sa_, bank(bla)[:, 0:ncol], AF.Sigmoid, bias=bm[:, l, mf:mf + 1], scale=1.0))
                    f.op(ACT, bkeys(blb) + ["bm"], [("hbB", sg)],
                         lambda: nc.scalar.activation(sb_, bank(blb)[:, 0:ncol], AF.Sigmoid, bias=bm[:, l, 8 + mf:8 + mf + 1], scale=1.0))
                    si = rot_stg.next()
                    t1 = stg[si][:, 0:ncol]
                    t2 = stg[si][:, 512:512 + ncol]
                    f.op(DVE, bkeys(bba) + [("hbA", sg)], [("stgA", si), ("stg", si)],
                         lambda: nc.vector.tensor_tensor(t1, bank(bba)[:, 0:ncol], sa_, ALU.mult))
                    f.op(DVE, bkeys(bbb) + [("hbB", sg)], [("stgB", si), ("hb", sg)],
                         lambda: nc.vector.tensor_tensor(t2, bank(bbb)[:, 0:ncol], sb_, ALU.mult))
                    mdst = mTs[:, mf, :] if smp else mT[:, mf, nt * 512:(nt + 1) * 512]
                    mkey = "mTs" if smp else ("mT", mf)
                    f.op(DVE, [("stgA", si), ("stgB", si)], [mkey, ("stg", si)],
                         lambda: nc.vector.tensor_tensor(mdst, t1, t2, ALU.add))

        wo = [e_bufs[2], wpbi]
        for h2 in range(2):
            f.dma(POOL, Wc[wo[h2]][:, :, 0:512], wout_d[l, :, :, h2 * 512:(h2 + 1) * 512], [], [("Wc", wo[h2])])

        preload_chunk(l + 1, OFF_A, 384, avoid=(wo[0], wo[1]))

        def phaseF1(P, lhs_fn, lhs_keys, x_ap, xkey, tmp_ap, tmp_key, out_ap, out_key):
            di = rot_d.next()

            def mmF():
                for kt in range(8):
                    nc.tensor.matmul(psA[di][0:P, 0:512], lhs_fn(kt), Wc[wo[0]][:, kt, 0:512], start=(kt == 0), stop=(kt == 7))
                for kt in range(8):
                    ins = nc.tensor.matmul(psA[di][0:P, 512:1024], lhs_fn(kt), Wc[wo[1]][:, kt, 0:512], start=(kt == 0), stop=(kt == 7))
                return ins
            dk = bkeys(2 * di) + bkeys(2 * di + 1)
            f.op(PE, lhs_keys + [("Wc", wo[0]), ("Wc", wo[1])], dk, mmF)
            ss = acc_stat()
            f.op(ACT, dk + ["stat"], [("st", ss), "junk"],
                 lambda: nc.scalar.activation(junk[0:P, :], psA[di][0:P, :], AF.Square, accum_out=stat[0:P, ss:ss + 1]))
            rs = rstd_from_ss(ss, 1.0 / D, P)
            f.op(DVE, dk + [("st", rs), "gpost"], [tmp_key],
                 lambda: nc.vector.scalar_tensor_tensor(tmp_ap, psA[di][0:P, :], stat[0:P, rs:rs + 1], gpost[0:P, :], ALU.mult, ALU.mult))
            if P == 128:
                f.op(POOL, [tmp_key, xkey], [out_key], lambda: nc.gpsimd.tensor_tensor(out_ap, tmp_ap, x_ap, ALU.add))
            else:
                f.op(DVE, [tmp_key, xkey], [out_key], lambda: nc.vector.tensor_tensor(out_ap, tmp_ap, x_ap, ALU.add))
        phaseF = phaseF1

        xsrc = xp if l == 0 else xscr
        GA_KEYS = [("gaT", a, n) for a in range(2) for n in range(4)]
        f.sync_on(ACT, GA_KEYS + UG_KEYS)

        def load_x(tt):
            xi = rot_stg.next()
            f.dma(SP, stg[xi][:], xsrc[tt * 128:(tt + 1) * 128, :], [("xscr", tt)] if l > 0 else [], [("stg", xi)])
            return xi
        xnext = load_x(0)
        st_f1, st_f2 = {}, {}

        def F1(tt):
            nonlocal_x = st_f1.pop(("x", tt))
            oi = rot_stg.next()
            phaseF(128, lambda kt: mT[:, kt, tt * 128:(tt + 1) * 128], MT_KEYS,
                   stg[nonlocal_x][:], ("stg", nonlocal_x), stg[oi][:], ("stg", oi), stg[oi][:], ("stg", oi))
            if last:
                f.dma(SP, y_p[tt * 128:(tt + 1) * 128, :], stg[oi][:], [("stg", oi)], [], is_output=True)
            else:
                f.dma(SP, xscr[tt * 128:(tt + 1) * 128, :], stg[oi][:], [("stg", oi)], [("xscr", tt)], semkey=("xscr", tt))
            st_f1[tt] = oi

        def F2(tt):
            oi = st_f1.pop(tt)
            st_f2[tt] = prenorm_a(stg[oi][:], ("stg", oi), 128)

        def F3(tt):
            prenorm_b(l + 1, st_f2.pop(tt), 128, hT[:, :, tt * 128:(tt + 1) * 128], ("hT", tt))

        for i in range(16 + 4):
            if i < 16:
                st_f1[("x", i)] = xnext
                if i + 1 < 16:
                    xnext = load_x(i + 1)
                F1(i)
            if not last:
                if 0 <= i - 2 < 16:
                    F2(i - 2)
                if 0 <= i - 4 < 16:
                    F3(i - 4)
        oi = rot_stg.next()
        phaseF(NB, lambda kt: mTs[:, kt, :], ["mTs"], xs_t[:], "xs_t", stg[oi][0:NB, :], ("stg", oi), xs_t[:], "xs_t")
        if last:
            f.dma(SP, y_s, xs_t[:], ["xs_t"], [], is_output=True)
        else:
            prenorm(l + 1, xs_t[:], "xs_t", NB, hTs[:], "hTs")

    f.finish()
    f.close()
    return nc, f


_PROG = {}


def _col_perm():
    cols = []
    for sp in range(2):
        for g in range(3):
            for base in (0, 768, 1536):
                cols.extend(range(base + g * 256 + sp * 128, base + g * 256 + sp * 128 + 128))
    cols.extend(range(2304, 2560))
    cols.extend(range(3328, 4096))
    cols.extend(range(2560, 3328))
    cols.extend(range(4096, 4864))
    for qd in range(4):
        cols.extend(range(4864 + qd * 256, 4864 + qd * 256 + 256))
        cols.extend(range(5888 + qd * 256, 5888 + qd * 256 + 256))
    assert len(cols) == N_IN
    return np.asarray(cols)


def _rope_table(pos):
    half = 8
    inv_freq = np.power(np.float32(500000.0), -np.arange(half, dtype=np.float32) / np.float32(half)).astype(np.float32)
    ang = pos.astype(np.float32)[..., None] * inv_freq
    c = np.cos(ang).astype(np.float32)
    s = np.sin(ang).astype(np.float32)
    return np.concatenate([c, c, s, s], axis=-1)


def _constants():
    ident = np.eye(128, dtype=np.float32)
    j = np.arange(128)[:, None]
    i = np.arange(128)[None, :]
    mask = np.concatenate([(j >= i), (j <= i)], axis=1).astype(np.float32)
    maskbias = ((mask - 1.0) * 30000.0).astype(np.float32)
    cs = np.zeros((3, 128, 16, 32), np.float32)
    for g, dl in enumerate(DIL):
        nblk = (S // dl) // 128
        for pt in range(16):
            r, mb = divmod(pt, nblk)
            pos = r + dl * (mb * 128 + np.arange(128))
            cs[g, :, pt, :] = _rope_table(pos)
    css = np.repeat(_rope_table(np.asarray([8192]))[0][None], NB, axis=0)
    sel = np.zeros((NB, NB, 128), np.float32)
    oh = np.zeros((128, NB, NB), np.float32)
    for b in range(NB):
        sel[b, b, :] = 1.0
        oh[:, b, b] = 1.0
    return dict(ident=ident, mask=mask, maskbias=maskbias, cs=cs, css=css.astype(np.float32), sel=sel, oh=oh)


def kernel(x_prompt, x_sample, cache_kv_w128, cache_kv_w512, cache_kv_w2048, norm_pre, w_in, b_merge,
           v_norm_g, v_norm_b, w_spatial, b_spatial, w_proj_a, w_proj_b, w_out, norm_post, _nlayers=DEPTH, _cores=8, _trace=False):
    f32 = np.float32
    if _nlayers not in _PROG:
        _PROG[_nlayers] = build_program(_nlayers)[0]
    nc = _PROG[_nlayers]
    ncores = _cores
    perm = _col_perm()
    w_in_l = np.ascontiguousarray(np.asarray(w_in, f32)[:, :, perm].reshape(DEPTH, 8, 128, N_IN).transpose(0, 2, 1, 3))
    wpa = np.ascontiguousarray(np.asarray(w_proj_a, f32).reshape(DEPTH, 2, 128, D).transpose(0, 2, 1, 3))
    wpb = np.ascontiguousarray(np.asarray(w_proj_b, f32).reshape(DEPTH, 6, 128, D).transpose(0, 2, 1, 3))
    wout = np.ascontiguousarray(np.asarray(w_out, f32).reshape(DEPTH, 8, 128, D).transpose(0, 2, 1, 3))
    gpreT = np.ascontiguousarray(np.asarray(norm_pre, f32).reshape(DEPTH, 8, 128).transpose(2, 0, 1))
    bm = np.ascontiguousarray(np.asarray(b_merge, f32).reshape(DEPTH, 16, 128).transpose(2, 0, 1))
    wsT = np.ascontiguousarray(np.asarray(w_spatial, f32).transpose(0, 3, 1, 2))
    grp = (np.arange(768) // 192).reshape(6, 128)
    bsT = np.ascontiguousarray(np.asarray(b_spatial, f32)[:, grp, :].transpose(0, 2, 1, 3))
    ws00 = np.ascontiguousarray(np.asarray(w_spatial, f32)[:, :, 0, 0].reshape(-1))
    bs0 = np.ascontiguousarray(np.asarray(b_spatial, f32)[:, :, 0].reshape(-1))
    consts = _constants()
    shared = dict(w_in=w_in_l, wpa=wpa, wpb=wpb, wout=wout, gpreT=gpreT, gpost=np.asarray(norm_post, f32),
                  bm=bm, gam=np.asarray(v_norm_g, f32), beta=np.asarray(v_norm_b, f32), wsT=wsT, bsT=bsT,
                  ws00=ws00, bs0=bs0, **consts)
    xp = np.asarray(x_prompt, f32)
    xs = np.asarray(x_sample, f32)
    c128 = np.asarray(cache_kv_w128, f32).reshape(DEPTH, 32, 128, 512)
    c512 = np.asarray(cache_kv_w512, f32).reshape(DEPTH, 32, 512, 512)
    c2048 = np.asarray(cache_kv_w2048, f32).reshape(DEPTH, 32, 2048, 512)
    in_maps = []
    for c in range(ncores):
        m = dict(shared)
        m["xp"] = np.ascontiguousarray(xp[c])
        m["xs"] = np.ascontiguousarray(xs[NB * c:NB * c + NB, 0, :])
        m["c128"] = np.ascontiguousarray(c128[:, NB * c:NB * c + NB])
        m["c512"] = np.ascontiguousarray(c512[:, NB * c:NB * c + NB])
        m["c2048"] = np.ascontiguousarray(c2048[:, NB * c:NB * c + NB])
        in_maps.append(m)
    res = run_bass_kernel_spmd(nc, in_maps, core_ids=list(range(ncores)), **({"trace": True} if _trace else {}))
    if _trace:
        print("EXEC_TIME_NS", res.exec_time_ns)
    R = res.results
    y_prompt = np.stack([R[c]["y_p"] for c in range(ncores)], axis=0).astype(f32)
    y_sample = np.concatenate([R[c]["y_s"] for c in range(ncores)], axis=0).reshape(NB * ncores, 1, D).astype(f32)
    outs = [y_prompt, y_sample]
    for name, keep in (("kv128p", 128), ("kv512p", 512), ("kv2048p", 2048)):
        a = np.stack([R[c][name] for c in range(ncores)], axis=1)
        outs.append(a.reshape(DEPTH, ncores, keep, 2, 4, 64).astype(f32))
    for name in ("kv128s", "kv512s", "kv2048s"):
        a = np.concatenate([R[c][name] for c in range(ncores)], axis=1)
        outs.append(a.reshape(DEPTH, NB * ncores, 1, 2, 4, 64).astype(f32))
    a = np.concatenate([R[c]["gv_s"] for c in range(ncores)], axis=1)
    outs.append(a.reshape(DEPTH, NB * ncores, 1, 768).astype(f32))
    return tuple(outs)
```
